# Optimizing a Trainium2 kernel written in Bass

```python
import math
import jax, jax.numpy as jnp
from jax import lax
import numpy as np

D_MODEL = 2048
BATCH = 4
SEQ = 2048
DEPTH = 4

GRID_W = 64
CTX_LEN = 256
D_POOL = D_MODEL // 4
POOL_WINDOWS = (2, 4, 8, 16)
N_POOL_GROUPS = len(POOL_WINDOWS)
POOL_GROUP = D_POOL // N_POOL_GROUPS
D_HYENA = D_MODEL // 4
FILTER_EMB = 33
FILTER_BANDS = (FILTER_EMB - 1) // 2
FILTER_ORDER = 64
FILTER_DECAY_TARGET = 1e-2
FILTER_FAST_PCT = 0.3
FILTER_SLOW_PCT = 1.5
RET_HEAD_DIM = 256
D_RET = D_MODEL // 2
RET_HEADS = D_RET // RET_HEAD_DIM
RET_CHUNK = 128
ROPE_BASE = 10000.0
ROPE_PAIRS = RET_HEAD_DIM // 4
N_BRANCH = 3
D_FF = 4 * D_MODEL
LN_EPS = 1e-5
GN_EPS = 1e-6
DEEPNORM_ALPHA = (2 * DEPTH) ** 0.25
DEEPNORM_BETA = (8 * DEPTH) ** -0.25
O_POOL = 0
O_HY = O_POOL + D_POOL
O_Q = O_HY + 3 * D_HYENA
O_K = O_Q + D_RET
O_V = O_K + D_RET
O_G = O_V + D_RET
O_GATE = O_G + D_RET
D_IN = O_GATE + N_BRANCH * D_MODEL

kernel_name = 'hybrid_pool_hyena_retention_dit_block'


def layer_norm(x, g, b):
    xf = x.astype(jnp.float32)
    mu = xf.mean(-1, keepdims=True)
    var = jnp.square(xf - mu).mean(-1, keepdims=True)
    return ((xf - mu) * lax.rsqrt(var + LN_EPS) * g + b).astype(x.dtype)


def modulate(h, shift, scale):
    return h * (1.0 + scale) + shift


def pool_mixer(u, w, scale):
    B, L, _ = u.shape
    ug = u.astype(jnp.float32).reshape(B, L, N_POOL_GROUPS, POOL_GROUP)
    csum = jnp.concatenate([jnp.zeros_like(ug[:, :1]), jnp.cumsum(ug, axis=1)], axis=1)
    t = jnp.arange(L)[:, None]
    win = jnp.array(POOL_WINDOWS)[None, :]
    lo = jnp.clip(t - win // 2, 0, L)
    hi = jnp.clip(t + win - win // 2, 0, L)
    grp = jnp.arange(N_POOL_GROUPS)[None, :]
    wsum = csum[:, hi, grp] - csum[:, lo, grp]
    pooled = wsum / (hi - lo).astype(jnp.float32)[None, :, :, None] - ug
    y = jnp.einsum('blgc,gcd->blgd', pooled, w)
    return y.reshape(B, L, D_POOL) * scale


def short_conv(u, w, b):
    up = jnp.pad(u, ((0, 0), (1, 1), (0, 0)))
    return up[:, :-2] * w[0] + up[:, 1:-1] * w[1] + up[:, 2:] * w[2] + b


def hyena_filters(L, p):
    t = jnp.linspace(0.0, 1.0, L, dtype=jnp.float32)[:, None]
    w = 2.0 * math.pi * jnp.arange(L, dtype=jnp.float32)[:, None] / L
    f = jnp.linspace(1e-4, FILTER_BANDS - 1, FILTER_BANDS, dtype=jnp.float32)[None, :]
    z = jnp.concatenate([t, jnp.cos(f * w), -jnp.sin(f * w)], axis=-1)
    hdn = jnp.sin(p['filt_f1'] * (z @ p['filt_w1'] + p['filt_b1']))
    hdn = jnp.sin(p['filt_f2'] * (hdn @ p['filt_w2'] + p['filt_b2']))
    hdn = jnp.sin(p['filt_f3'] * (hdn @ p['filt_w3'] + p['filt_b3']))
    h = (hdn @ p['filt_w4']).astype(jnp.float32)
    max_decay = math.log(FILTER_DECAY_TARGET) / FILTER_FAST_PCT
    min_decay = math.log(FILTER_DECAY_TARGET) / FILTER_SLOW_PCT
    deltas = jnp.linspace(min_decay, max_decay, D_HYENA, dtype=jnp.float32)
    decay = jnp.exp(-t * jnp.abs(deltas)[None, :])
    h = h * jnp.concatenate([decay, decay], axis=-1)
    return h[:, :D_HYENA], h[:, D_HYENA:]


def bidir_long_conv(u, h_f, h_b):
    L = u.shape[1]
    k2 = jnp.concatenate([h_f, jnp.zeros_like(h_f[:1]), h_b[:0:-1]], axis=0)
    U = jnp.fft.rfft(u, n=2 * L, axis=1)
    K = jnp.fft.rfft(k2, n=2 * L, axis=0)
    return jnp.fft.irfft(U * K[None], n=2 * L, axis=1)[:, :L]


def hyena_mixer(u, p):
    L = u.shape[1]
    z = short_conv(u, p['conv_w'], p['conv_b']).astype(jnp.float32)
    v, x0, x1 = jnp.split(z, 3, axis=-1)
    h_f, h_b = hyena_filters(L, p)
    uu = v * x1
    return (bidir_long_conv(uu, h_f, h_b) + uu * p['hyena_d']) * x0


def grid_rope_tables(L):
    rows = L // GRID_W
    row = jnp.repeat(jnp.arange(rows, dtype=jnp.float32), GRID_W)
    col = jnp.tile(jnp.arange(GRID_W, dtype=jnp.float32), rows)
    inv = ROPE_BASE ** (-jnp.arange(ROPE_PAIRS, dtype=jnp.float32) / ROPE_PAIRS)
    ang_r = row[:, None] * inv[None, :]
    ang_c = col[:, None] * inv[None, :]
    return (jnp.cos(ang_r), jnp.sin(ang_r), jnp.cos(ang_c), jnp.sin(ang_c))


def rotate(x, cos, sin):
    x1, x2 = jnp.split(x, 2, axis=-1)
    return jnp.concatenate([x1 * cos - x2 * sin, x2 * cos + x1 * sin], axis=-1)


def apply_grid_rope(x, rope):
    cr, sr, cc, sc = rope
    half = RET_HEAD_DIM // 2
    return jnp.concatenate([rotate(x[..., :half], cr, sr), rotate(x[..., half:], cc, sc)], axis=-1)


def to_heads(t):
    B, L, _ = t.shape
    return t.reshape(B, L, RET_HEADS, RET_HEAD_DIM).transpose(0, 2, 1, 3).astype(jnp.float32)


def log_decays(param):
    lg = jnp.log1p(-jnp.exp(param.astype(jnp.float32)))
    return lg[0], lg[1]


def chunk_retention(q, k, v, lg, state0):
    B, H, L, _ = q.shape
    dv = v.shape[-1]
    n = L // RET_CHUNK

    def chunks(t):
        return t.reshape(B, H, n, RET_CHUNK, t.shape[-1]).transpose(2, 0, 1, 3, 4)

    idx = jnp.arange(RET_CHUNK, dtype=jnp.float32)
    rel = idx[:, None] - idx[None, :]
    lower = rel >= 0
    dmask = jnp.where(lower[None], jnp.exp(jnp.where(lower, rel, 0.0)[None] * lg[:, None, None]), 0.0)
    q_dec = jnp.exp((idx + 1.0)[None, :] * lg[:, None])[None, :, :, None]
    k_dec = jnp.exp((RET_CHUNK - 1.0 - idx)[None, :] * lg[:, None])[None, :, :, None]
    c_dec = jnp.exp(RET_CHUNK * lg)[None, :, None, None]

    def step(state, xs):
        qc, kc, vc = xs
        scores = jnp.einsum('bhid,bhjd->bhij', qc, kc) * dmask
        out = (jnp.einsum('bhij,bhjv->bhiv', scores, vc)
               + jnp.einsum('bhid,bhdv->bhiv', qc * q_dec, state))
        state = state * c_dec + jnp.einsum('bhjd,bhjv->bhdv', kc * k_dec, vc)
        return state, out

    _, out = lax.scan(step, state0, (chunks(q), chunks(k), chunks(v)))
    return out.transpose(1, 2, 0, 3, 4).reshape(B, H, L, dv)


def bidir_retention(q, k, v, lg_f, lg_b, s_f, s_b):
    def flip(t):
        return jnp.flip(t, axis=2)
    o_f = chunk_retention(q, k, v, lg_f, s_f)
    o_b = flip(chunk_retention(flip(q), flip(k), flip(v), lg_b, s_b))
    return o_f + o_b


def context_states(k, v, lg_f, lg_b):
    L = k.shape[2]
    pos = jnp.arange(L, dtype=jnp.float32)
    w_f = jnp.exp((L - 1.0 - pos)[None, :] * lg_f[:, None])
    w_b = jnp.exp(pos[None, :] * lg_b[:, None])
    s_f = jnp.einsum('bhlk,hl,bhlv->bhkv', k, w_f, v)
    s_b = jnp.einsum('bhlk,hl,bhlv->bhkv', k, w_b, v)
    return s_f, s_b


def hybrid_mixer(h, p, rope, s_f, s_b):
    B, L, _ = h.shape
    z = h @ p['w_in'] + p['b_in']
    y_a = pool_mixer(z[..., O_POOL:O_HY], p['pool_w'], p['pool_scale'])
    y_b = hyena_mixer(z[..., O_HY:O_Q], p)
    q = to_heads(z[..., O_Q:O_K])
    k = to_heads(z[..., O_K:O_V]) * RET_HEAD_DIM ** -0.5
    v = to_heads(z[..., O_V:O_G])
    if rope is not None:
        q = apply_grid_rope(q, rope)
        k = apply_grid_rope(k, rope)
    lg_f, lg_b = log_decays(p['ret_decay'])
    o = bidir_retention(q, k, v, lg_f, lg_b, s_f, s_b)
    mu = o.mean(-1, keepdims=True)
    var = jnp.square(o - mu).mean(-1, keepdims=True)
    o = ((o - mu) * lax.rsqrt(var + GN_EPS)).transpose(0, 2, 1, 3).reshape(B, L, D_RET)
    y_c = jax.nn.silu(z[..., O_G:O_GATE].astype(jnp.float32)) * o
    gates = jax.nn.sigmoid(z[..., O_GATE:].astype(jnp.float32)).reshape(B, L, N_BRANCH, D_MODEL)
    merged = (gates[:, :, 0] * (y_a @ p['p_a'])
              + gates[:, :, 1] * (y_b @ p['p_b'])
              + gates[:, :, 2] * (y_c @ p['p_c']))
    return merged @ p['w_o'] + p['b_o'], k, v


def context_kv(h, p):
    z = h @ p['w_in'][:, O_K:O_G] + p['b_in'][O_K:O_G]
    return to_heads(z[..., :D_RET]) * RET_HEAD_DIM ** -0.5, to_heads(z[..., D_RET:])


def sq_relu_mlp(h, p):
    a = jax.nn.relu(h @ p['w_mlp1'] + p['b_mlp1'])
    return jnp.square(a) @ p['w_mlp2'] + p['b_mlp2']


def setup_inputs(seed: int = 0) -> dict:
    key = jax.random.key(seed)
    ks = jax.random.split(key, 64)
    counter = [0]

    def nrm(shape, scale):
        k = ks[counter[0]]
        counter[0] += 1
        return jax.random.normal(k, shape, jnp.float32) * scale

    beta = DEEPNORM_BETA
    ret_base = -(5.0 + jnp.arange(RET_HEADS, dtype=jnp.float32)) * math.log(2.0)
    return {
        'x': nrm((BATCH, SEQ, D_MODEL), 1.0),
        'c': nrm((BATCH, D_MODEL), 1.0),
        'ctx': nrm((BATCH, CTX_LEN, D_MODEL), 1.0),
        'c_ctx': nrm((D_MODEL,), 1.0),
        'w_ada': nrm((DEPTH, D_MODEL, 6 * D_MODEL), 0.5 * D_MODEL ** -0.5),
        'b_ada': nrm((DEPTH, 6 * D_MODEL), 0.02),
        'w_in': nrm((DEPTH, D_MODEL, D_IN), D_MODEL ** -0.5),
        'b_in': nrm((DEPTH, D_IN), 0.02),
        'conv_w': nrm((DEPTH, 3, 3 * D_HYENA), 3.0 ** -0.5),
        'conv_b': nrm((DEPTH, 3 * D_HYENA), 0.02),
        'pool_w': nrm((DEPTH, N_POOL_GROUPS, POOL_GROUP, POOL_GROUP), POOL_GROUP ** -0.5),
        'pool_scale': 1.0 + nrm((DEPTH, D_POOL), 0.02),
        'filt_w1': nrm((DEPTH, FILTER_EMB, FILTER_ORDER), FILTER_EMB ** -0.5),
        'filt_b1': nrm((DEPTH, FILTER_ORDER), 0.02),
        'filt_f1': 1.0 + nrm((DEPTH, FILTER_ORDER), 0.1),
        'filt_w2': nrm((DEPTH, FILTER_ORDER, FILTER_ORDER), FILTER_ORDER ** -0.5),
        'filt_b2': nrm((DEPTH, FILTER_ORDER), 0.02),
        'filt_f2': 1.0 + nrm((DEPTH, FILTER_ORDER), 0.1),
        'filt_w3': nrm((DEPTH, FILTER_ORDER, FILTER_ORDER), FILTER_ORDER ** -0.5),
        'filt_b3': nrm((DEPTH, FILTER_ORDER), 0.02),
        'filt_f3': 1.0 + nrm((DEPTH, FILTER_ORDER), 0.1),
        'filt_w4': nrm((DEPTH, FILTER_ORDER, 2 * D_HYENA), 0.1 * FILTER_ORDER ** -0.5),
        'hyena_d': nrm((DEPTH, D_HYENA), 0.5),
        'ret_decay': ret_base + nrm((DEPTH, 2, RET_HEADS), 0.05),
        'p_a': nrm((DEPTH, D_POOL, D_MODEL), beta * D_POOL ** -0.5),
        'p_b': nrm((DEPTH, D_HYENA, D_MODEL), beta * D_HYENA ** -0.5),
        'p_c': nrm((DEPTH, D_RET, D_MODEL), beta * D_RET ** -0.5),
        'w_o': nrm((DEPTH, D_MODEL, D_MODEL), beta * D_MODEL ** -0.5),
        'b_o': nrm((DEPTH, D_MODEL), 0.02),
        'ln1_g': 1.0 + nrm((DEPTH, D_MODEL), 0.02),
        'ln1_b': nrm((DEPTH, D_MODEL), 0.02),
        'w_mlp1': nrm((DEPTH, D_MODEL, D_FF), D_MODEL ** -0.5),
        'b_mlp1': nrm((DEPTH, D_FF), 0.02),
        'w_mlp2': nrm((DEPTH, D_FF, D_MODEL), beta * D_FF ** -0.5),
        'b_mlp2': nrm((DEPTH, D_MODEL), 0.02),
        'ln2_g': 1.0 + nrm((DEPTH, D_MODEL), 0.02),
        'ln2_b': nrm((DEPTH, D_MODEL), 0.02),
    }


def reference(x, c, ctx, c_ctx, w_ada, b_ada, w_in, b_in, conv_w, conv_b, pool_w, pool_scale,
              filt_w1, filt_b1, filt_f1, filt_w2, filt_b2, filt_f2, filt_w3, filt_b3, filt_f3, filt_w4,
              hyena_d, ret_decay, p_a, p_b, p_c, w_o, b_o, ln1_g, ln1_b,
              w_mlp1, b_mlp1, w_mlp2, b_mlp2, ln2_g, ln2_b):
    L = x.shape[1]
    rope = grid_rope_tables(L)
    silu_c = jax.nn.silu(c)
    silu_cc = jax.nn.silu(c_ctx)
    zero_state = jnp.zeros((ctx.shape[0], RET_HEADS, RET_HEAD_DIM, RET_HEAD_DIM), jnp.float32)
    for l in range(DEPTH):
        p = {
            'w_in': w_in[l], 'b_in': b_in[l], 'conv_w': conv_w[l], 'conv_b': conv_b[l],
            'pool_w': pool_w[l], 'pool_scale': pool_scale[l],
            'filt_w1': filt_w1[l], 'filt_b1': filt_b1[l], 'filt_f1': filt_f1[l],
            'filt_w2': filt_w2[l], 'filt_b2': filt_b2[l], 'filt_f2': filt_f2[l],
            'filt_w3': filt_w3[l], 'filt_b3': filt_b3[l], 'filt_f3': filt_f3[l],
            'filt_w4': filt_w4[l], 'hyena_d': hyena_d[l], 'ret_decay': ret_decay[l],
            'p_a': p_a[l], 'p_b': p_b[l], 'p_c': p_c[l], 'w_o': w_o[l], 'b_o': b_o[l],
            'w_mlp1': w_mlp1[l], 'b_mlp1': b_mlp1[l], 'w_mlp2': w_mlp2[l], 'b_mlp2': b_mlp2[l],
        }
        mod_x = (silu_c @ w_ada[l] + b_ada[l])[:, None, :]
        mod_c = (silu_cc @ w_ada[l] + b_ada[l])[None, None, :]
        sh1, sc1, g1, sh2, sc2, g2 = jnp.split(mod_x, 6, axis=-1)
        csh1, csc1, cg1, csh2, csc2, cg2 = jnp.split(mod_c, 6, axis=-1)
        last = l == DEPTH - 1
        hc = modulate(ctx, csh1, csc1)
        if not last:
            yc, kc, vc = hybrid_mixer(hc, p, None, zero_state, zero_state)
        else:
            kc, vc = context_kv(hc, p)
        lg_f, lg_b = log_decays(p['ret_decay'])
        s_f, s_b = context_states(kc, vc, lg_f, lg_b)
        yx, _, _ = hybrid_mixer(modulate(x, sh1, sc1), p, rope, s_f, s_b)
        x = layer_norm(DEEPNORM_ALPHA * x + g1 * yx, ln1_g[l], ln1_b[l])
        x = layer_norm(DEEPNORM_ALPHA * x + g2 * sq_relu_mlp(modulate(x, sh2, sc2), p), ln2_g[l], ln2_b[l])
        if not last:
            ctx = layer_norm(DEEPNORM_ALPHA * ctx + cg1 * yc, ln1_g[l], ln1_b[l])
            ctx = layer_norm(DEEPNORM_ALPHA * ctx + cg2 * sq_relu_mlp(modulate(ctx, csh2, csc2), p),
                             ln2_g[l], ln2_b[l])
    return x
```

```python
import math
from contextlib import ExitStack
import numpy as np
import concourse.bass as bass
import concourse.mybir as mybir
from concourse.bass_utils import run_bass_kernel_spmd

F32 = mybir.dt.float32
BF16 = mybir.dt.bfloat16
AF = mybir.ActivationFunctionType
ALU = mybir.AluOpType

D = 2048
L = 2048
LC = 256
HALF = 1024
NT = HALF + LC
DEPTH = 4
DIN = 12288
O_HY, O_Q, O_K, O_V, O_G, O_GATE = 512, 2048, 3072, 4096, 5120, 6144
DFF = 8192
ALPHA = (2 * DEPTH) ** 0.25
NPL = 420
C_BADA, C_BIN, C_BO, C_L1G, C_L1B, C_B2, C_L2G, C_L2B, C_B1 = 0, 96, 192, 208, 224, 240, 256, 272, 288
C_CB, C_CW, C_PS, C_FILT, C_RET = 352, 364, 400, 404, 410
TWO_PI = 2.0 * math.pi
RELW = 3584
DEBUG_OUT = []


class Sched:
    def __init__(self, nc, stack, n_dma_sems=6):
        self.nc = nc
        self.eng = {'pe': nc.tensor, 'act': nc.scalar, 'dve': nc.vector,
                    'pool': nc.gpsimd, 'sp': nc.sync}
        self.sem = {k: stack.enter_context(nc.semaphore(f"s_{k}")) for k in self.eng}
        self.cnt = {k: 0 for k in self.eng}
        self.seen = {k: {} for k in self.eng}
        self.last_w = {}
        self.readers = {}
        self.dsem = {}
        for q in ('sp', 'pool', 'act'):
            self.dsem[q] = [[stack.enter_context(nc.semaphore(f"d_{q}{i}")), 0]
                            for i in range(4 if q == 'pool' else n_dma_sems)]
        self.dnext = {q: 0 for q in self.dsem}
        self.csem = stack.enter_context(nc.semaphore("c_sem"))
        self.ccnt = 0

    def _wait(self, ek, h):
        sem, val, src = h
        if src == ek and ek == 'pe':
            return
        sid = id(sem)
        if self.seen[ek].get(sid, 0) >= val:
            return
        self.eng[ek].wait_ge(sem, val)
        self.seen[ek][sid] = val

    def _deps(self, ek, reads, writes):
        for r in reads:
            h = self.last_w.get(r)
            if h is not None:
                self._wait(ek, h)
        for w in writes:
            h = self.last_w.get(w)
            if h is not None:
                self._wait(ek, h)
            for h in self.readers.get(w, ()):
                self._wait(ek, h)

    def _commit(self, h, reads, writes):
        for w in writes:
            self.last_w[w] = h
            self.readers[w] = []
        for r in reads:
            if r in writes:
                continue
            self.readers.setdefault(r, []).append(h)

    def op(self, ek, fn, reads=(), writes=()):
        self._deps(ek, reads, writes)
        ins = fn(self.eng[ek])
        self.cnt[ek] += 1
        ins.then_inc(self.sem[ek], 1)
        h = (self.sem[ek], self.cnt[ek], ek)
        self._commit(h, reads, writes)
        return h

    def mmv(self, instrs, reads=(), writes=()):
        self._deps('pe', reads, writes)
        ins = None
        for (o, l, r, st, sp) in instrs:
            ins = self.nc.tensor.matmul(o, l, r, start=st, stop=sp)
        self.cnt['pe'] += 1
        ins.then_inc(self.sem['pe'], 1)
        h = (self.sem['pe'], self.cnt['pe'], 'pe')
        self._commit(h, reads, writes)
        return h

    def mm(self, out_ap, pairs, reads=(), writes=()):
        n = len(pairs)
        return self.mmv([(out_ap, l, r, i == 0, i == n - 1) for i, (l, r) in enumerate(pairs)],
                        reads, writes)

    def dma(self, q, out, in_, reads=(), writes=(), **kw):
        slot = self.dsem[q][self.dnext[q]]
        self.dnext[q] = (self.dnext[q] + 1) % len(self.dsem[q])
        sem, uses = slot
        if uses:
            self._wait(q, (sem, 16 * uses, 'dma'))
        self._deps(q, reads, writes)
        self.eng[q].dma_start(out=out, in_=in_, **kw).then_inc(sem, 16)
        slot[1] += 1
        h = (sem, 16 * slot[1], 'dma')
        self._commit(h, reads, writes)
        return h

    def coll(self, fn, reads=(), writes=()):
        self._deps('pool', reads, writes)
        self.ccnt += 1
        fn(self.eng['pool']).then_inc(self.csem, 16)
        h = (self.csem, 16 * self.ccnt, 'dma')
        self._commit(h, reads, writes)
        return h

    def barrier(self):
        hs = []
        for q in self.dsem:
            for sem, uses in self.dsem[q]:
                if uses:
                    hs.append((sem, 16 * uses, 'dma'))
        for k in self.eng:
            if self.cnt[k]:
                hs.append((self.sem[k], self.cnt[k], 'x'))
        if self.ccnt:
            hs.append((self.csem, 16 * self.ccnt, 'dma'))
        for k in self.eng:
            for h in hs:
                self._wait(k, h)
        self.last_w = {}
        self.readers = {}


def build_program(depth=DEPTH, debug_out=(), stop_after=None, ncores=8):
    nc = bass.Bass("TRN2", target_bir_lowering=False)

    def din(name, shape, dt=F32):
        return nc.dram_tensor(name, list(shape), dt, kind="ExternalInput").ap()

    def dscr(name, shape, dt):
        kind = "ExternalOutput" if name in debug_out else "Internal"
        return nc.dram_tensor(name, list(shape), dt, kind=kind).ap()

    xin = din("xin", [D, NT])
    cvec = din("cvec", [128, 16, 2])
    w_ada = din("w_ada", [depth, D, 6 * D])
    w_in = din("w_in", [depth, D, DIN])
    pool_w = din("pool_w", [depth, 4, 128, 128])
    fw1 = din("filt_w1", [depth, 33, 64])
    fw2 = din("filt_w2", [depth, 64, 64])
    fw3 = din("filt_w3", [depth, 64, 64])
    fw4 = din("filt_w4", [depth, 64, 1024])
    p_a = din("p_a", [depth, 512, D])
    p_b = din("p_b", [depth, 512, D])
    p_c = din("p_c", [depth, 1024, D])
    w_o = din("w_o", [depth, D, D])
    w_m1 = din("w_mlp1", [depth, D, DFF])
    w_m2 = din("w_mlp2", [depth, DFF, D])
    plin = din("pl", [128, DEPTH, NPL])
    bvbc = din("bvbc", [depth, 128, 1024])
    hyd = din("hyd", [depth, 1, 512])
    zft = din("zft", [33, L])
    zftc = din("zftc", [33, LC])
    tnin = din("tn", [128, 18])
    nadin = din("nad", [128, 512])
    fm = din("fm", [16, 128, 16, 256])
    fmc = din("fmc", [2, 128, 2, 256])
    gm = din("gm", [32, 128, HALF])
    gmc = din("gmc", [4, 128, LC])
    ropeg = din("ropeg", [128, 4, L])
    ropeo = din("ropeo", [128, 4, HALF])
    pmin = din("pm", [128, 128])
    identin = din("ident", [128, 128])
    pinv = din("pinv", [128, 4, NT])
    halom = din("halom", [128, 2])
    relT = din("relT", [128, RELW])
    relC = din("relC", [128, 384])
    yout = nc.dram_tensor("yout", [D, HALF], F32, kind="ExternalOutput").ap()

    xres = dscr("xres", [D, NT], F32)
    hx = [dscr(f"hx{c}", [D // 2, HALF], BF16) for c in range(2)]
    hfull = [dscr(f"hfull{c}", [D, HALF], BF16) for c in range(2)]
    hctx = dscr("hctx", [D, LC], BF16)
    z1 = dscr("z1", [2, D, HALF], F32)
    z1c = dscr("z1c", [D, LC], F32)
    zk = dscr("zk", [1024, L + LC], BF16)
    zv = dscr("zv", [L + LC, 1024], BF16)
    zq = dscr("zq", [1024, NT], BF16)
    zg = dscr("zg", [1024, NT], BF16)
    ycat = dscr("ycat", [D, NT], BF16)
    uut = dscr("uut", [L + LC, 512], BF16)
    x0c = dscr("x0c", [512, NT], F32)
    mg = dscr("mg", [D, NT], BF16)
    h2 = dscr("h2", [D, NT], BF16)

    _uc = [0]

    def uq(name):
        _uc[0] += 1
        return f"{name}_{_uc[0]}"

    with ExitStack() as st0:
        S = Sched(nc, st0)
        sb0 = lambda name, shape, dt: st0.enter_context(nc.sbuf_tensor(uq(name), list(shape), dt))
        cst = sb0("cst", [128, 8], F32)
        mod = sb0("mod", [128, DEPTH, 96, 2], F32)
        mod1 = sb0("mod1", [128, DEPTH, 96, 2], F32)
        pl = sb0("plt", [128, DEPTH, NPL], F32)
        identb = sb0("identb", [128, 128], BF16)
        pm = sb0("pmt", [128, 128], F32)
        onesd = sb0("onesd", [128, 128], F32)
        der = sb0("der", [128, 64], F32)
        hm = sb0("hm", [128, 2], F32)
        ZERO = cst[:, 0:1]
        NEGPI = cst[:, 1:2]

        S.op('dve', lambda e: e.memset(cst[:, 0:1], 0.0), writes=['cst'])
        S.op('dve', lambda e: e.memset(cst[:, 1:2], -math.pi), writes=['cst'])
        S.op('dve', lambda e: e.memset(cst[:, 2:3], 1.0), writes=['cst'])
        S.op('dve', lambda e: e.memset(cst[:, 3:4], 1e-5), writes=['cst'])
        S.op('dve', lambda e: e.memset(cst[:, 4:5], 1e-6), writes=['cst'])
        S.op('dve', lambda e: e.memset(onesd[:], 1.0 / D), writes=['onesd'])
        S.dma('sp', pl[:], plin, writes=['pl'])
        S.dma('sp', pm[:], pmin, writes=['pm'])
        S.dma('sp', hm[:], halom, writes=['hm'])
        S.dma('pool', identb[:], identin, writes=['identb'])
        S.dma('sp', xres, xin, writes=['xres'])
        S.barrier()

        def act(out, in_, func, bias=None, scale=1.0, reads=(), writes=()):
            b = ZERO[0:in_.shape[0], :] if bias is None else bias
            return S.op('act', lambda e: e.activation(out=out, in_=in_, func=func, bias=b, scale=scale),
                        reads, writes)

        def tt(ek, out, a, b, op, reads=(), writes=()):
            return S.op(ek, lambda e: e.tensor_tensor(out=out, in0=a, in1=b, op=op), reads, writes)

        def ts(ek, out, a, s1, s2, op0, op1=None, reads=(), writes=()):
            if op1 is None:
                return S.op(ek, lambda e: e.tensor_scalar(out=out, in0=a, scalar1=s1, scalar2=None, op0=op0),
                            reads, writes)
            return S.op(ek, lambda e: e.tensor_scalar(out=out, in0=a, scalar1=s1, scalar2=s2, op0=op0, op1=op1),
                        reads, writes)

        def stt(ek, out, a, s, b, op0, op1, reads=(), writes=()):
            return S.op(ek, lambda e: e.scalar_tensor_tensor(out=out, in0=a, scalar=s, in1=b, op0=op0, op1=op1),
                        reads, writes)

        def cp(ek, out, a, reads=(), writes=()):
            if ek == 'act':
                return act(out, a, AF.Identity, reads=reads, writes=writes)
            return S.op(ek, lambda e: e.tensor_copy(out=out, in_=a), reads, writes)

        TB3 = [(0, 512, 0), (512, 512, 0), (1024, 256, 1)]

        def stage_ada():
            with ExitStack() as st:
                sb = lambda name, shape, dt: st.enter_context(nc.sbuf_tensor(uq(name), list(shape), dt))
                wada = sb("wada", [128, 2, 16, 512], BF16)
                cv = sb("cv", [128, 16, 2], F32)
                scv = sb("scv", [128, 16, 2], BF16)
                ps = st.enter_context(nc.psum_tensor(uq("psada"), [128, 96, 2], F32))
                S.dma('sp', cv[:], cvec, writes=['cv'])
                act(scv[:], cv[:], AF.Silu, reads=['cv'], writes=['scv'])
                def load(i):
                    if i >= depth * 24:
                        return
                    l_, c_ = divmod(i, 24)
                    S.dma('pool', wada[:, i % 2], w_ada[l_, :, c_ * 512:(c_ + 1) * 512].rearrange("(kt p) c -> p kt c", p=128),
                          writes=[('wada', i % 2)])
                ci = 0
                load(0)
                for l in range(depth):
                    for c in range(24):
                        slot = ci % 2
                        ci += 1
                        load(ci)
                        for ct in range(4):
                            T = c * 4 + ct
                            S.mm(ps[:, T, :], [(wada[:, slot, kt, ct * 128:(ct + 1) * 128], scv[:, kt, :]) for kt in range(16)],
                                 reads=[('wada', slot), 'scv'], writes=['psada'])
                    for r in range(2):
                        tt('dve', mod[:, l, :, r], ps[:, :, r], pl[:, l, C_BADA:C_BADA + 96], ALU.add,
                           reads=['psada', 'pl'], writes=[('mod', l, r)])
                        ts('dve', mod1[:, l, :, r], mod[:, l, :, r], 1.0, None, ALU.add,
                           reads=[('mod', l, r)], writes=[('mod1', l, r)])
            S.barrier()

        def stage_h(l):
            with ExitStack() as st:
                sb = lambda name, shape, dt: st.enter_context(nc.sbuf_tensor(uq(name), list(shape), dt))
                xb = sb("xb", [128, 2, 16, 512], F32)
                hb = sb("hb", [128, 2, 16, 512], BF16)
                for i, (t0, n, row) in enumerate(TB3):
                    slot = i % 2
                    S.dma('sp', xb[:, slot, :, :n], xres[:, t0:t0 + n].rearrange("(nt p) t -> p nt t", p=128),
                          reads=['xres'], writes=[('xb', slot)])
                    for nt in range(16):
                        if nt % 2 == 0:
                            act(hb[:, slot, nt, :n], xb[:, slot, nt, :n], AF.Identity, bias=mod[:, l, nt, row:row + 1],
                                scale=mod1[:, l, 16 + nt, row:row + 1], reads=[('xb', slot)], writes=[('hb', slot, nt)])
                        else:
                            ts('dve', hb[:, slot, nt, :n], xb[:, slot, nt, :n], mod1[:, l, 16 + nt, row:row + 1],
                               mod[:, l, nt, row:row + 1], ALU.mult, ALU.add, reads=[('xb', slot)], writes=[('hb', slot, nt)])
                    rk = [('hb', slot, nt) for nt in range(16)]
                    if row == 0:
                        for c in range(2):
                            S.dma('sp', hx[c][:, t0:t0 + n].rearrange("(nt p) t -> p nt t", p=128), hb[:, slot, c * 8:(c + 1) * 8, :n],
                                  reads=rk, writes=['hx'])
                    else:
                        S.dma('sp', hctx.rearrange("(nt p) t -> p nt t", p=128), hb[:, slot, :, :n], reads=rk, writes=['hctx'])
                groups = [[2 * i, 2 * i + 1] for i in range(ncores // 2)]
                for c in range(2):
                    S.op('pool', lambda e, c=c: e.collective_compute("AllGather", ALU.bypass, replica_groups=groups,
                                                                      ins=[hx[c]], outs=[hfull[c]]),
                         reads=['hx'], writes=['hfull'])
            S.barrier()

        def stage_derive(l):
            for r in range(2):
                tt('dve', der[:, r * 16:(r + 1) * 16], mod[:, l, 32:48, r], pl[:, l, C_BO:C_BO + 16], ALU.mult, writes=[('der', r)])
                tt('dve', der[:, 32 + r * 16:32 + (r + 1) * 16], mod[:, l, 80:96, r], pl[:, l, C_B2:C_B2 + 16], ALU.mult,
                   writes=[('der', 2 + r)])
            S.barrier()

        def stage_inproj_a(l):
            with ExitStack() as st:
                sb = lambda name, shape, dt: st.enter_context(nc.sbuf_tensor(uq(name), list(shape), dt))
                hall = sb("hall", [128, 16, L + LC], BF16)
                wch = sb("wch", [128, 2, 16, 512], BF16)
                rope = sb("rope", [128, 4, L], F32)
                zst = sb("zst", [128, 2, 4, 512], F32)
                kf = sb("kf", [128, 2, 512], F32)
                t1 = sb("t1", [128, 2, 512], F32)
                ko = sb("ko", [128, 2, 4, 512], BF16)
                vo = sb("vo", [128, 2, 512], BF16)
                bvb = sb("bvb", [128, 1024], F32)
                bk16 = sb("bk16", [128, 8], F32)
                psm = [st.enter_context(nc.psum_tensor(uq(f"psm{i}"), [128, 512], F32)) for i in range(4)]
                psr = [st.enter_context(nc.psum_tensor(uq(f"psr{i}"), [128, 512], F32)) for i in range(2)]
                for r in range(2):
                    for c in range(2):
                        S.dma('sp', hall[:, c * 8:(c + 1) * 8, r * HALF:(r + 1) * HALF],
                              hfull[c][r * (D // 2):(r + 1) * (D // 2), :].rearrange("(kt p) t -> p kt t", p=128),
                              reads=['hfull'], writes=['hall'])
                S.dma('sp', hall[:, :, L:L + LC], hctx.rearrange("(kt p) t -> p kt t", p=128), reads=['hctx'], writes=['hall'])
                S.dma('sp', rope[:], ropeg, writes=['rope'])
                S.dma('sp', bvb[:], bvbc[l], writes=['bvb'])
                ts('dve', bk16[:], pl[:, l, C_BIN + 24:C_BIN + 32], 1.0 / 16.0, None, ALU.mult, writes=['bk16'])
                TBA = [(0, 512), (512, 512), (1024, 512), (1536, 512), (2048, 256)]
                chunks = [('z', 0), ('z', 512), ('z', 1024), ('z', 1536), ('k', O_K), ('k', O_K + 512), ('v', O_V), ('v', O_V + 512)]
                pi = 0
                si = 0
                def load(i):
                    if i < len(chunks):
                        cc0 = chunks[i][1]
                        S.dma('pool', wch[:, i % 2], w_in[l, :, cc0:cc0 + 512].rearrange("(kt p) c -> p kt c", p=128),
                              writes=[('wch', i % 2)])
                load(0)
                for ci, (kind, c0) in enumerate(chunks):
                    slot = ci % 2
                    load(ci + 1)
                    if kind == 'v':
                        cv0 = c0 - O_V
                        for tti in range(18):
                            p = pi % 4
                            pi += 1
                            S.mm(psm[p][:], [(hall[:, kt, tti * 128:(tti + 1) * 128], wch[:, slot, kt, :]) for kt in range(16)],
                                 reads=[('wch', slot), 'hall'], writes=[('psm', p)])
                            vs = tti % 2
                            tt('dve', vo[:, vs, :], psm[p][:], bvb[:, cv0:cv0 + 512], ALU.add,
                               reads=[('psm', p), 'bvb'], writes=[('vo', vs)])
                            S.dma('sp', zv[tti * 128:(tti + 1) * 128, cv0:cv0 + 512], vo[:, vs, :], reads=[('vo', vs)], writes=['zv'])
                        continue
                    for (t0, n) in TBA:
                        ss = si % 2
                        si += 1
                        for ct in range(4):
                            p = pi % 4
                            pi += 1
                            gcol = (c0 // 128) + ct
                            S.mm(psm[p][:, :n], [(wch[:, slot, kt, ct * 128:(ct + 1) * 128], hall[:, kt, t0:t0 + n]) for kt in range(16)],
                                 reads=[('wch', slot), 'hall'], writes=[('psm', p)])
                            if kind == 'z':
                                act(zst[:, ss, ct, :n], psm[p][:, :n], AF.Identity, bias=pl[:, l, C_BIN + gcol:C_BIN + gcol + 1],
                                    reads=[('psm', p)], writes=[('zst', ss, ct)])
                            else:
                                kt8 = gcol - 24
                                fs = (si + ct) % 2
                                act(kf[:, fs, :n], psm[p][:, :n], AF.Identity, bias=bk16[:, kt8:kt8 + 1], scale=1.0 / 16.0,
                                    reads=[('psm', p), 'bk16'], writes=[('kf', fs)])
                                if t0 < L:
                                    var = kt8 % 2
                                    S.mm(psr[fs][:, :n], [(pm[:], kf[:, fs, :n])], reads=[('kf', fs)], writes=[('psr', fs)])
                                    tt('dve', t1[:, fs, :n], kf[:, fs, :n], rope[:, var, t0:t0 + n], ALU.mult,
                                       reads=[('kf', fs), 'rope'], writes=[('t1', fs)])
                                    tt('dve', kf[:, fs, :n], psr[fs][:, :n], rope[:, 2 + var, t0:t0 + n], ALU.mult,
                                       reads=[('psr', fs), 'rope'], writes=[('kf', fs)])
                                    tt('pool', ko[:, ss, ct, :n], t1[:, fs, :n], kf[:, fs, :n], ALU.add,
                                       reads=[('t1', fs), ('kf', fs)], writes=[('ko', ss, ct)])
                                else:
                                    cp('pool', ko[:, ss, ct, :n], kf[:, fs, :n], reads=[('kf', fs)], writes=[('ko', ss, ct)])
                        if kind == 'z':
                            if t0 < L:
                                dst = z1[t0 // HALF, c0:c0 + 512, (t0 % HALF):(t0 % HALF) + n]
                            else:
                                dst = z1c[c0:c0 + 512, :]
                            S.dma('sp', dst.rearrange("(ct p) t -> p ct t", p=128), zst[:, ss, :, :n],
                                  reads=[('zst', ss, ct) for ct in range(4)], writes=['z1'])
                        else:
                            r0 = c0 - O_K
                            S.dma('sp', zk[r0:r0 + 512, t0:t0 + n].rearrange("(ct p) t -> p ct t", p=128), ko[:, ss, :, :n],
                                  reads=[('ko', ss, ct) for ct in range(4)], writes=['zk'])
            S.barrier()

        def stage_inproj_b(l):
            with ExitStack() as st:
                sb = lambda name, shape, dt: st.enter_context(nc.sbuf_tensor(uq(name), list(shape), dt))
                hoc = sb("hoc", [128, 16, NT], BF16)
                wch = sb("wch", [128, 2, 16, 512], BF16)
                rope = sb("rope", [128, 4, HALF], F32)
                kf = sb("kf", [128, 2, 512], F32)
                t1 = sb("t1", [128, 2, 512], F32)
                ko = sb("ko", [128, 2, 4, 512], BF16)
                psm = [st.enter_context(nc.psum_tensor(uq(f"psm{i}"), [128, 512], F32)) for i in range(4)]
                psr = [st.enter_context(nc.psum_tensor(uq(f"psr{i}"), [128, 512], F32)) for i in range(2)]
                for c in range(2):
                    S.dma('sp', hoc[:, c * 8:(c + 1) * 8, 0:HALF], hx[c].rearrange("(kt p) t -> p kt t", p=128), reads=['hx'], writes=['hoc'])
                S.dma('sp', hoc[:, :, HALF:NT], hctx.rearrange("(kt p) t -> p kt t", p=128), reads=['hctx'], writes=['hoc'])
                S.dma('sp', rope[:], ropeo, writes=['rope'])
                chunks = [('q', O_Q), ('q', O_Q + 512), ('g', O_G), ('g', O_G + 512)]
                pi = 0
                si = 0
                def load(i):
                    if i < len(chunks):
                        cc0 = chunks[i][1]
                        S.dma('pool', wch[:, i % 2], w_in[l, :, cc0:cc0 + 512].rearrange("(kt p) c -> p kt c", p=128),
                              writes=[('wch', i % 2)])
                load(0)
                for ci, (kind, c0) in enumerate(chunks):
                    slot = ci % 2
                    load(ci + 1)
                    for (t0, n, row) in TB3:
                        ss = si % 2
                        si += 1
                        for ct in range(4):
                            p = pi % 4
                            pi += 1
                            gcol = (c0 // 128) + ct
                            bias = pl[:, l, C_BIN + gcol:C_BIN + gcol + 1]
                            S.mm(psm[p][:, :n], [(wch[:, slot, kt, ct * 128:(ct + 1) * 128], hoc[:, kt, t0:t0 + n]) for kt in range(16)],
                                 reads=[('wch', slot), 'hoc'], writes=[('psm', p)])
                            if kind == 'g':
                                act(ko[:, ss, ct, :n], psm[p][:, :n], AF.Silu, bias=bias, reads=[('psm', p)], writes=[('ko', ss, ct)])
                                continue
                            fs = (si + ct) % 2
                            act(kf[:, fs, :n], psm[p][:, :n], AF.Identity, bias=bias, reads=[('psm', p)], writes=[('kf', fs)])
                            if row == 0:
                                var = gcol % 2
                                S.mm(psr[fs][:, :n], [(pm[:], kf[:, fs, :n])], reads=[('kf', fs)], writes=[('psr', fs)])
                                tt('dve', t1[:, fs, :n], kf[:, fs, :n], rope[:, var, t0:t0 + n], ALU.mult,
                                   reads=[('kf', fs), 'rope'], writes=[('t1', fs)])
                                tt('dve', kf[:, fs, :n], psr[fs][:, :n], rope[:, 2 + var, t0:t0 + n], ALU.mult,
                                   reads=[('psr', fs), 'rope'], writes=[('kf', fs)])
                                tt('pool', ko[:, ss, ct, :n], t1[:, fs, :n], kf[:, fs, :n], ALU.add,
                                   reads=[('t1', fs), ('kf', fs)], writes=[('ko', ss, ct)])
                            else:
                                cp('pool', ko[:, ss, ct, :n], kf[:, fs, :n], reads=[('kf', fs)], writes=[('ko', ss, ct)])
                        dstt = zq if kind == 'q' else zg
                        r0 = c0 - (O_Q if kind == 'q' else O_G)
                        S.dma('sp', dstt[r0:r0 + 512, t0:t0 + n].rearrange("(ct p) t -> p ct t", p=128), ko[:, ss, :, :n],
                              reads=[('ko', ss, ct) for ct in range(4)], writes=['zq'])
            S.barrier()

        pid_sp = nc.sync.partition_id()
        s_own = pid_sp % 2
        s_par = (pid_sp + 1) % 2

        def stage_pool(l):
            with ExitStack() as st:
                sb = lambda name, shape, dt: st.enter_context(nc.sbuf_tensor(uq(name), list(shape), dt))
                W = HALF + 16
                WC = LC + 16
                u = sb("pu", [128, 4, W], F32)
                uc = sb("puc", [128, 4, WC], F32)
                ta = sb("pta", [128, 4, W], F32)
                tb_ = sb("ptb", [128, 4, W], F32)
                pw = sb("ppw", [128, 4, 128], F32)
                pwb = sb("ppwb", [128, 4, 128], BF16)
                pin = sb("ppin", [128, 4, NT], F32)
                pooled = sb("ppooled", [128, 4, NT], BF16)
                yo = sb("pyo", [128, 4, NT], BF16)
                ps = [st.enter_context(nc.psum_tensor(uq(f"pps{i}"), [128, 512], F32)) for i in range(4)]
                S.dma('sp', pw[:], pool_w[l].rearrange("g c d -> c g d"), writes=['pw'])
                cp('dve', pwb[:], pw[:], reads=['pw'], writes=['pwb'])
                S.dma('sp', pin[:], pinv, writes=['pin'])
                S.op('pool', lambda e: e.memset(uc[:], 0.0), writes=['uc'])
                S.dma('sp', u[:, :, 8:8 + HALF], z1[bass.ts(s_own, 1), 0:512, :].rearrange("o (g p) t -> p (o g) t", p=128),
                      reads=['z1'], writes=['u'])
                S.dma('sp', u[:, :, 0:8], z1[bass.ts(s_par, 1), 0:512, HALF - 8:HALF].rearrange("o (g p) t -> p (o g) t", p=128),
                      reads=['z1'], writes=['uL'])
                S.dma('sp', u[:, :, 8 + HALF:W], z1[bass.ts(s_par, 1), 0:512, 0:8].rearrange("o (g p) t -> p (o g) t", p=128),
                      reads=['z1'], writes=['uR'])
                ts('dve', u[:, :, 0:8], u[:, :, 0:8], hm[:, 0:1], None, ALU.mult, reads=['uL'], writes=['uL'])
                ts('dve', u[:, :, 8 + HALF:W], u[:, :, 8 + HALF:W], hm[:, 1:2], None, ALU.mult, reads=['uR'], writes=['uR'])
                S.dma('sp', uc[:, :, 8:8 + LC], z1c[0:512, :].rearrange("(g p) t -> p g t", p=128), reads=['z1', 'uc'], writes=['uc'])
                for (src, Wd, n, o0, key) in ((u, W, HALF, 0, 'u'), (uc, WC, LC, HALF, 'uc')):
                    rd = ['u', 'uL', 'uR'] if key == 'u' else ['uc']
                    for g in range(4):
                        eng = 'dve' if g % 2 == 0 else 'pool'
                        cur, curk = src, None
                        bufs = [ta, tb_]
                        steps = [(1, 0), (1, -1), (2, -2), (4, -4)][:g + 1]
                        offs = [(1, 0), (1, 1), (2, 2), (4, 4)][:g + 1]
                        lo, hi = 0, Wd
                        for k, (am, bp) in enumerate(offs):
                            nb = bufs[k % 2]
                            nlo, nhi = lo + am, hi - bp
                            tt(eng, nb[:, g, nlo:nhi], cur[:, g, nlo - am:nhi - am], cur[:, g, nlo + bp:nhi + bp], ALU.add,
                               reads=rd + [('pt', g)], writes=[('pt', g)])
                            cur = nb
                            lo, hi = nlo, nhi
                        assert lo <= 8 and hi >= 8 + n
                        ob = bufs[(g + 1) % 2]
                        tt(eng, ob[:, g, 8:8 + n], cur[:, g, 8:8 + n], pin[:, g, o0:o0 + n], ALU.mult,
                           reads=[('pt', g), 'pin'], writes=[('pt', g)])
                        tt(eng, pooled[:, g, o0:o0 + n], ob[:, g, 8:8 + n], src[:, g, 8:8 + n], ALU.subtract,
                           reads=rd + [('pt', g)], writes=[('pooled', g, o0)])
                pi = 0
                for g in range(4):
                    for (t0, n, row) in TB3:
                        p = pi % 4
                        pi += 1
                        S.mm(ps[p][:, :n], [(pwb[:, g, :], pooled[:, g, t0:t0 + n])],
                             reads=['pwb', ('pooled', g, 0), ('pooled', g, HALF)], writes=[('pps', p)])
                        act(yo[:, g, t0:t0 + n], ps[p][:, :n], AF.Identity, scale=pl[:, l, C_PS + g:C_PS + g + 1],
                            reads=[('pps', p)], writes=[('yo', g)])
                S.dma('sp', ycat[0:512, :].rearrange("(g p) t -> p g t", p=128), yo[:],
                      reads=[('yo', g) for g in range(4)], writes=['ycat_a'])
            S.barrier()

        def stage_hyfront(l):
            with ExitStack() as st:
                sb = lambda name, shape, dt: st.enter_context(nc.sbuf_tensor(uq(name), list(shape), dt))
                WL = L + 2
                WC = LC + 2
                zin = sb("hzin", [128, 2, WL], F32)
                zc = sb("hzc", [128, 2, WL], F32)
                uu = sb("huu", [128, WL], BF16)
                uts = sb("huts", [128, 18, 512], BF16)
                x0i = sb("hx0i", [128, HALF + 2], F32)
                x0ci = sb("hx0ci", [128, WC], F32)
                x0o = sb("hx0o", [128, 4, NT], F32)
                pst = [st.enter_context(nc.psum_tensor(uq(f"hpt{i}"), [128, 512], BF16)) for i in range(2)]

                def conv(eng, out, src, n, ch):
                    w = lambda k: pl[:, l, C_CW + k * 12 + ch:C_CW + k * 12 + ch + 1]
                    b = pl[:, l, C_CB + ch:C_CB + ch + 1]
                    return [(out, src, w, b, n)]

                def do_conv(out, src, n, ch, rkeys, wkey):
                    w = lambda k: pl[:, l, C_CW + k * 12 + ch:C_CW + k * 12 + ch + 1]
                    b = pl[:, l, C_CB + ch:C_CB + ch + 1]
                    act(out[:, 0:n], src[:, 1:n + 1], AF.Identity, bias=b, scale=w(1), reads=rkeys, writes=[wkey])
                    stt('dve', out[:, 0:n], src[:, 0:n], w(0), out[:, 0:n], ALU.mult, ALU.add, reads=rkeys + [wkey], writes=[wkey])
                    stt('dve', out[:, 0:n], src[:, 2:n + 2], w(2), out[:, 0:n], ALU.mult, ALU.add, reads=rkeys + [wkey], writes=[wkey])

                for ct in range(4):
                    for j, hyt in enumerate((ct, 8 + ct)):
                        r0 = O_HY + hyt * 128
                        S.op('pool', lambda e, j=j: e.memset(zin[:, j, 0:1], 0.0), writes=[('zin', j)])
                        S.op('pool', lambda e, j=j: e.memset(zin[:, j, WL - 1:WL], 0.0), reads=[('zin', j)], writes=[('zin', j)])
                        for r in range(2):
                            S.dma('sp', zin[:, j, 1 + r * HALF:1 + (r + 1) * HALF], z1[r, r0:r0 + 128, :], reads=['z1', ('zin', j)],
                                  writes=[('zin', j)])
                        do_conv(zc[:, j, :], zin[:, j, :], L, hyt, [('zin', j)], ('zc', j))
                    tt('pool', uu[:, 0:L], zc[:, 0, 0:L], zc[:, 1, 0:L], ALU.mult, reads=[('zc', 0), ('zc', 1)], writes=['uu'])
                    for tti in range(16):
                        S.op('pe', lambda e, tti=tti: e.transpose(pst[tti % 2][:, ct * 128:(ct + 1) * 128], uu[:, tti * 128:(tti + 1) * 128], identb[:]),
                             reads=['uu', 'identb'], writes=[('pst', tti % 2)])
                        cp('act' if tti % 2 else 'dve', uts[:, tti, ct * 128:(ct + 1) * 128], pst[tti % 2][:, ct * 128:(ct + 1) * 128],
                           reads=[('pst', tti % 2)], writes=[('uts', tti, ct)])
                    for j, hyt in enumerate((ct, 8 + ct)):
                        r0 = O_HY + hyt * 128
                        S.op('pool', lambda e, j=j: e.memset(zin[:, j, 0:1], 0.0), reads=[('zin', j), ('zc', j)], writes=[('zin', j)])
                        S.op('pool', lambda e, j=j: e.memset(zin[:, j, WC - 1:WC], 0.0), reads=[('zin', j)], writes=[('zin', j)])
                        S.dma('sp', zin[:, j, 1:1 + LC], z1c[r0:r0 + 128, :], reads=['z1', ('zin', j)], writes=[('zin', j)])
                        do_conv(zc[:, j, :], zin[:, j, :], LC, hyt, [('zin', j)], ('zc', j))
                    tt('pool', uu[:, 0:LC], zc[:, 0, 0:LC], zc[:, 1, 0:LC], ALU.mult, reads=[('zc', 0), ('zc', 1), 'uu'], writes=['uu'])
                    for tti in range(2):
                        S.op('pe', lambda e, tti=tti: e.transpose(pst[tti % 2][:, ct * 128:(ct + 1) * 128], uu[:, tti * 128:(tti + 1) * 128], identb[:]),
                             reads=['uu', 'identb'], writes=[('pst', tti % 2)])
                        cp('act' if tti % 2 else 'dve', uts[:, 16 + tti, ct * 128:(ct + 1) * 128], pst[tti % 2][:, ct * 128:(ct + 1) * 128],
                           reads=[('pst', tti % 2)], writes=[('uts', 16 + tti, ct)])
                    hyt = 4 + ct
                    r0 = O_HY + hyt * 128
                    S.dma('sp', x0i[:, 1:1 + HALF], z1[bass.ts(s_own, 1), r0:r0 + 128, :].rearrange("o p t -> p (o t)"),
                          reads=['z1', 'x0c_i'], writes=['x0i'])
                    S.dma('sp', x0i[:, 0:1], z1[bass.ts(s_par, 1), r0:r0 + 128, HALF - 1:HALF].rearrange("o p t -> p (o t)"),
                          reads=['z1', 'x0c_i'], writes=['x0L'], allow_slow_non_contiguous=True)
                    S.dma('sp', x0i[:, HALF + 1:HALF + 2], z1[bass.ts(s_par, 1), r0:r0 + 128, 0:1].rearrange("o p t -> p (o t)"),
                          reads=['z1', 'x0c_i'], writes=['x0R'], allow_slow_non_contiguous=True)
                    ts('dve', x0i[:, 0:1], x0i[:, 0:1], hm[:, 0:1], None, ALU.mult, reads=['x0L'], writes=['x0L'])
                    ts('dve', x0i[:, HALF + 1:HALF + 2], x0i[:, HALF + 1:HALF + 2], hm[:, 1:2], None, ALU.mult, reads=['x0R'], writes=['x0R'])
                    do_conv(x0o[:, ct, 0:HALF], x0i[:], HALF, hyt, ['x0i', 'x0L', 'x0R'], ('x0o', ct))
                    S.op('pool', lambda e: e.memset(x0ci[:, 0:1], 0.0), reads=['x0ci'], writes=['x0ci'])
                    S.op('pool', lambda e: e.memset(x0ci[:, WC - 1:WC], 0.0), reads=['x0ci'], writes=['x0ci'])
                    S.dma('sp', x0ci[:, 1:1 + LC], z1c[r0:r0 + 128, :], reads=['z1', 'x0ci'], writes=['x0ci'])
                    do_conv(x0o[:, ct, HALF:NT], x0ci[:], LC, hyt, ['x0ci'], ('x0o', ct))
                    S.op('pool', lambda e: e.memset(x0i[:, 0:1], 0.0), reads=[('x0o', ct), 'x0i', 'x0L', 'x0R'], writes=['x0c_i'])
                S.dma('sp', uut.rearrange("(tt p) c -> p tt c", p=128), uts[:],
                      reads=[('uts', t, c) for t in range(18) for c in range(4)], writes=['uut'])
                S.dma('sp', x0c.rearrange("(ct p) t -> p ct t", p=128), x0o[:], reads=[('x0o', c) for c in range(4)], writes=['x0c'])
            S.barrier()

        def stage_hyena(l):
            with ExitStack() as st:
                sb = lambda name, shape, dt: st.enter_context(nc.sbuf_tensor(uq(name), list(shape), dt))
                w1 = sb("fw1", [33, 64], F32)
                w2 = sb("fw2", [64, 64], F32)
                w3 = sb("fw3", [64, 64], F32)
                w4 = sb("fw4", [64, 1024], F32)
                zf = sb("fzf", [33, L], F32)
                ha = sb("fha", [64, L], F32)
                hb_ = sb("fhb", [64, L], F32)
                tmp = sb("ftmp", [64, 2, 512], F32)
                tmpm = sb("ftmpm", [64, 1, 512], F32)
                fb = sb("ffb", [64, 3], F32)
                tn = sb("ftn", [128, 18], F32)
                nad = sb("fnad", [128, 512], F32)
                hdr = sb("fhdr", [1, 512], F32)
                dec = sb("fdec", [128, 1, 512], F32)
                hfb = sb("fhfb", [128, 1, 2, 512], F32)
                hs = sb("fhs", [128, 16, 512], BF16)
                hd = sb("fhd", [128, 16, 512], BF16)
                uts = sb("futs", [128, 16, 512], BF16)
                fch = sb("ffch", [128, 2, 16, 256], BF16)
                ahs = sb("fahs", [128, 2, 512], F32)
                pn = sb("fpn", [1, 512], F32)
                pt = sb("fpt", [128, 4, 512], F32)
                ypk = sb("fypk", [128, 32, 512], BF16)
                gch = sb("fgch", [128, 2, 2, HALF], BF16)
                x0t = sb("fx0t", [128, 4, NT], F32)
                ybo = sb("fybo", [128, 4, NT], BF16)
                ps = [st.enter_context(nc.psum_tensor(uq(f"fps{i}"), [128, 512], F32)) for i in range(8)]

                S.dma('sp', w1[:], fw1[l], writes=['w1'])
                S.dma('sp', w2[:], fw2[l], writes=['w2'])
                S.dma('sp', w3[:], fw3[l], writes=['w3'])
                S.dma('sp', w4[:], fw4[l], writes=['w4'])
                S.dma('sp', tn[:], tnin, writes=['tn'])
                S.dma('sp', nad[:], nadin, writes=['nad'])
                S.dma('sp', hdr[:], hyd[l], writes=['hdr'])
                S.dma('sp', x0t[:], x0c.rearrange("(ct p) t -> p ct t", p=128), reads=['x0c'], writes=['x0t'])
                for k in range(3):
                    tt('dve', fb[:, k:k + 1], pl[0:64, l, C_FILT + 2 * k:C_FILT + 2 * k + 1], pl[0:64, l, C_FILT + 2 * k + 1:C_FILT + 2 * k + 2],
                       ALU.mult, writes=[('fb', k)])

                def run(Lq, zsrc, tt0, npair, nst, fsrc, gsrc, nr, tok0, uut_rows, is_ctx):
                    nblk = (Lq + 511) // 512
                    S.dma('sp', zf[:, :Lq], zsrc, reads=['zf', 'ha'], writes=['zf'])
                    srcs = [(w1, zf, 33), (w2, ha, 64), (w3, hb_, 64)]
                    dsts = [ha, hb_, ha]
                    for k in range(3):
                        wk, src, kk = srcs[k]
                        dst = dsts[k]
                        fk = pl[0:64, l, C_FILT + 2 * k + 1:C_FILT + 2 * k + 2]
                        for bi in range(nblk):
                            n = min(512, Lq - bi * 512)
                            p = bi % 2
                            S.mm(ps[p][0:64, :n], [(wk[0:kk, :], src[0:kk, bi * 512:bi * 512 + n])],
                                 reads=['w1', 'w2', 'w3', 'zf', ('h', k - 1)], writes=[('fps', p)])
                            ts('dve', tmp[:, p, :n], ps[p][0:64, :n], fk, fb[:, k:k + 1], ALU.mult, ALU.add,
                               reads=[('fps', p), ('fb', k)], writes=[('tmp', p)])
                            for rnd in range(2):
                                for (cmpop, thr, addv) in ((ALU.is_gt, math.pi, -TWO_PI), (ALU.is_lt, -math.pi, TWO_PI)):
                                    ts('dve', tmpm[:, 0, :n], tmp[:, p, :n], thr, addv, cmpop, ALU.mult,
                                       reads=[('tmp', p), 'tmpm'], writes=['tmpm'])
                                    tt('dve', tmp[:, p, :n], tmp[:, p, :n], tmpm[:, 0, :n], ALU.add,
                                       reads=[('tmp', p), 'tmpm'], writes=[('tmp', p)])
                            act(dst[:, bi * 512:bi * 512 + n], tmp[:, p, :n], AF.Sin,
                                reads=[('tmp', p)], writes=[('h', k)])
                    for ti in range(Lq // 128):
                        p2 = 2 + 2 * (ti % 2)
                        q = 0
                        for hh in range(2):
                            S.mm(ps[p2 + hh][:], [(ha[:, ti * 128:(ti + 1) * 128], w4[:, hh * 512:(hh + 1) * 512])],
                                 reads=[('h', 2), 'w4'], writes=[('fps', p2 + hh)])
                        act(dec[:, q, :], nad[:], AF.Exp, scale=tn[:, tt0 + ti:tt0 + ti + 1], reads=['nad', 'tn'], writes=[('dec', q)])
                        for hh in range(2):
                            tt('dve', hfb[:, q, hh, :], ps[p2 + hh][:], dec[:, q, :], ALU.mult,
                               reads=[('fps', p2 + hh), ('dec', q)], writes=[('hfb', q, hh)])
                        if ti == 0:
                            S.op('dve', lambda e: e.memset(hfb[0:1, q, 1, :], 0.0), reads=[('hfb', q, 1)], writes=[('hfb', q, 1)])
                            tt('dve', hfb[0:1, q, 0, :], hfb[0:1, q, 0, :], hdr[:], ALU.add, reads=[('hfb', q, 0), 'hdr'], writes=[('hfb', q, 0)])
                        tt('pool', hs[:, ti, :], hfb[:, q, 0, :], hfb[:, q, 1, :], ALU.add,
                           reads=[('hfb', q, 0), ('hfb', q, 1)], writes=[('hs', ti)])
                        tt('pool', hd[:, ti, :], hfb[:, q, 1, :], hfb[:, q, 0, :], ALU.subtract,
                           reads=[('hfb', q, 0), ('hfb', q, 1)], writes=[('hd', ti)])
                    S.dma('sp', uts[:, 0:nst, :], uut[uut_rows:uut_rows + Lq, :].rearrange("(tt p) c -> p tt c", p=128),
                          reads=['uut', 'uts'], writes=['uts'])
                    hsk = [('hs', t) for t in range(nst)]
                    hdk = [('hd', t) for t in range(nst)]
                    def loadf(i):
                        if i < npair:
                            S.dma('pool', fch[:, i % 2, 0:nst, :], fsrc[i], writes=[('fch', i % 2)])
                    loadf(0)
                    for i in range(npair):
                        slot = i % 2
                        loadf(i + 1)
                        b0 = 4 * (i % 2)
                        S.mm(ps[b0 + 0][:], [(fch[:, slot, s_, 0:128], uts[:, s_, :]) for s_ in range(nst)],
                             reads=[('fch', slot), 'uts'], writes=[('fps', b0)])
                        S.mm(ps[b0 + 1][:], [(fch[:, slot, s_, 128:256], uts[:, s_, :]) for s_ in range(nst)],
                             reads=[('fch', slot), 'uts'], writes=[('fps', b0 + 1)])
                        S.mm(ps[b0 + 2][:], [(fch[:, slot, s_, 0:128], hs[:, s_, :]) for s_ in range(nst)],
                             reads=[('fch', slot)] + hsk, writes=[('fps', b0 + 2)])
                        S.mm(ps[b0 + 3][:], [(fch[:, slot, s_, 128:256], hd[:, s_, :]) for s_ in range(nst)],
                             reads=[('fch', slot)] + hdk, writes=[('fps', b0 + 3)])
                        act(ahs[:, 0, :], ps[b0 + 2][:], AF.Identity, reads=[('fps', b0 + 2)], writes=[('ahs', 0)])
                        act(ahs[:, 1, :], ps[b0 + 3][:], AF.Identity, reads=[('fps', b0 + 3)], writes=[('ahs', 1)])
                        tt('dve', pt[:, 0, :], ps[b0 + 0][:], ahs[:, 0, :], ALU.mult, reads=[('fps', b0), ('ahs', 0)], writes=[('pt', 0)])
                        tt('dve', pt[:, 1, :], ps[b0 + 1][:], ahs[:, 1, :], ALU.mult, reads=[('fps', b0 + 1), ('ahs', 1)], writes=[('pt', 1)])
                        tt('dve', pt[:, 2, :], ps[b0 + 0][:], ahs[:, 1, :], ALU.mult, reads=[('fps', b0), ('ahs', 1)], writes=[('pt', 2)])
                        tt('dve', pt[:, 3, :], ps[b0 + 1][:], ahs[:, 0, :], ALU.mult, reads=[('fps', b0 + 1), ('ahs', 0)], writes=[('pt', 3)])
                        tt('pool', ypk[:, i, :], pt[:, 0, :], pt[:, 1, :], ALU.add, reads=[('pt', 0), ('pt', 1)], writes=[('ypk', i)])
                        tt('pool', ypk[:, npair + i, :], pt[:, 2, :], pt[:, 3, :], ALU.subtract, reads=[('pt', 2), ('pt', 3)],
                           writes=[('ypk', npair + i)])
                        if i == 0:
                            S.mm(ps[b0 + 3][0:1, :], [(fch[:, slot, s_, 128:129], hs[:, s_, :]) for s_ in range(nst)],
                                 reads=[('fch', slot), ('ahs', 1), ('pt', 3)] + hsk, writes=[('fps', b0 + 3)])
                            act(pn[:], ps[b0 + 3][0:1, :], AF.Identity, reads=[('fps', b0 + 3)], writes=['pn'])
                            cp('pool', ypk[0:1, 0, :], pt[0:1, 0, :], reads=[('pt', 0), ('ypk', 0)], writes=[('ypk', 0)])
                            tt('dve', ypk[0:1, npair, :], ps[b0 + 1][0:1, :], pn[:], ALU.mult, reads=[('fps', b0 + 1), 'pn', ('ypk', npair)],
                               writes=[('ypk', npair)])
                    ntok = HALF if not is_ctx else LC
                    nblk_o = (ntok + 511) // 512
                    ypkk = [('ypk', r) for r in range(nr)]
                    if not is_ctx:
                        def loadg(i):
                            if i < nr // 2:
                                S.dma('pool', gch[:, i % 2], gsrc[i * 2:(i + 1) * 2].rearrange("r p t -> p r t"), writes=[('gch', i % 2)])
                        loadg(0)
                        for rc in range(nr // 2):
                            slot = rc % 2
                            loadg(rc + 1)
                            instrs = []
                            for rr in range(2):
                                r = rc * 2 + rr
                                for ct in range(4):
                                    for tb in range(2):
                                        instrs.append((ps[ct * 2 + tb][:], ypk[:, r, ct * 128:(ct + 1) * 128], gch[:, slot, rr, tb * 512:(tb + 1) * 512],
                                                       r == 0, r == nr - 1))
                            S.mmv(instrs, reads=[('gch', slot)] + ypkk, writes=[('fps', k) for k in range(8)])
                        for ct in range(4):
                            for tb in range(2):
                                tt('dve', ybo[:, ct, tb * 512:(tb + 1) * 512], ps[ct * 2 + tb][:], x0t[:, ct, tb * 512:(tb + 1) * 512], ALU.mult,
                                   reads=[('fps', ct * 2 + tb), 'x0t'], writes=[('ybo', ct)])
                    else:
                        for sl in range(2):
                            S.dma('pool', gch[:, sl, :, 0:LC], gsrc[sl * 2:(sl + 1) * 2].rearrange("r p t -> p r t"), writes=[('gch', sl)])
                        instrs = []
                        for r in range(nr):
                            for ct in range(4):
                                instrs.append((ps[ct][:, 0:LC], ypk[:, r, ct * 128:(ct + 1) * 128], gch[:, r // 2, r % 2, 0:LC], r == 0, r == nr - 1))
                        S.mmv(instrs, reads=[('gch', 0), ('gch', 1)] + ypkk, writes=[('fps', k) for k in range(4)])
                        for ct in range(4):
                            tt('dve', ybo[:, ct, HALF:NT], ps[ct][:, 0:LC], x0t[:, ct, HALF:NT], ALU.mult,
                               reads=[('fps', ct), 'x0t'], writes=[('ybo', ct)])

                run(L, zft, 0, 16, 16, fm, gm, 32, 0, 0, False)
                run(LC, zftc, 16, 2, 2, fmc, gmc, 4, HALF, L, True)
                S.dma('sp', ycat[512:1024, :].rearrange("(ct p) t -> p ct t", p=128), ybo[:], reads=[('ybo', c) for c in range(4)],
                      writes=['ycat_b'])
            S.barrier()

        def stage_ret(l):
            with ExitStack() as st:
                sb = lambda name, shape, dt: st.enter_context(nc.sbuf_tensor(uq(name), list(shape), dt))
                kh = sb("rkh", [128, 2, L + LC], BF16)
                vh = sb("rvh", [128, 18, 256], BF16)
                qh = sb("rqh", [128, 2, NT], BF16)
                gh = sb("rgh", [128, 2, NT], BF16)
                rel = sb("rrel", [128, RELW], F32)
                relc = sb("rrelc", [128, 384], F32)
                msk = sb("rmsk", [128, RELW], F32)
                mskc = sb("rmskc", [128, 384], F32)
                ta = sb("rta", [128, RELW], F32)
                relp = sb("rrelp", [128, RELW], F32)
                reln = sb("rreln", [128, RELW], F32)
                relcp = sb("rrelcp", [128, 384], F32)
                relcn = sb("rrelcn", [128, 384], F32)
                lg = sb("rlg", [128, 8], F32)
                ptT = sb("rpt", [128, 2, 512], BF16)
                stt_ = sb("rstat", [128, 2, 6], F32)
                mv = sb("rmv", [128, 2, 2], F32)
                rstd = sb("rrstd", [128, 2, 1], F32)
                on = sb("ron", [128, 2, 256], BF16)
                yco = sb("ryco", [128, 2, NT], BF16)
                oc = sb("roc", [128, 2, 4, 256], F32)
                pend = []
                bset = [0]
                pss = [st.enter_context(nc.psum_tensor(uq(f"rps{i}"), [128, 512], F32)) for i in range(2)]
                pso = [st.enter_context(nc.psum_tensor(uq(f"rpo{i}"), [128, 512], F32)) for i in range(4)]
                pstr = [st.enter_context(nc.psum_tensor(uq(f"rpt{i}"), [128, 2, 128], BF16)) for i in range(2)]
                def defer_finish(q0, nq):
                    s_ = bset[0]
                    bset[0] ^= 1
                    for qi in range(nq):
                        act(oc[:, s_, qi, :], pso[qi][:, 0:256], AF.Identity, reads=[('pso', qi)], writes=[('oc', s_, qi)])
                    while pend:
                        pend.pop(0)()
                    pend.append(lambda: finish_queries(q0, nq, oc, s_, stt_, mv, rstd, on, pstr, gh, yco))

                S.dma('sp', rel[:], relT, writes=['rel'])
                S.dma('sp', relc[:], relC, writes=['relc'])
                act(lg[:], pl[:, l, C_RET:C_RET + 8], AF.Exp, writes=['lg'])
                act(lg[:], lg[:], AF.Ln, bias=cst[:, 2:3], scale=-1.0, reads=['lg'], writes=['lg1'])
                ts('dve', relp[:], rel[:], 0.0, None, ALU.max, reads=['rel'], writes=['relp'])
                ts('pool', reln[:], rel[:], -1.0, 0.0, ALU.mult, ALU.max, reads=['rel'], writes=['reln'])
                ts('dve', relcp[:], relc[:], 0.0, None, ALU.max, reads=['relc'], writes=['relp'])
                ts('pool', relcn[:], relc[:], -1.0, 0.0, ALU.mult, ALU.max, reads=['relc'], writes=['reln'])
                for hh in range(4):
                    for (rsrc, rp, rn, mdst, Wd, key) in ((rel, relp, reln, msk, RELW, 'm'), (relc, relcp, relcn, mskc, 384, 'mc')):
                        act(ta[:, :Wd], rp[:, :Wd], AF.Identity, scale=lg[:, hh:hh + 1], reads=['relp', 'lg1', 'ta'], writes=['ta'])
                        stt('dve', ta[:, :Wd], rn[:, :Wd], lg[:, 4 + hh:5 + hh], ta[:, :Wd], ALU.mult, ALU.add,
                            reads=['reln', 'lg1', 'ta'], writes=['ta'])
                        act(ta[:, :Wd], ta[:, :Wd], AF.Exp, reads=['ta'], writes=['ta'])
                        stt('dve', mdst[:, :Wd], rsrc[:, :Wd], 0.0, ta[:, :Wd], ALU.is_equal, ALU.add,
                            reads=['ta', 'rel', 'relc', key], writes=[key])
                    S.dma('sp', kh[:], zk[hh * 256:(hh + 1) * 256, :].rearrange("(dt p) t -> p dt t", p=128), reads=['zk', 'kh'], writes=['kh'])
                    S.dma('sp', vh[:], zv[:, hh * 256:(hh + 1) * 256].rearrange("(tc p) c -> p tc c", p=128), reads=['zv', 'vh'], writes=['vh'])
                    S.dma('sp', qh[:], zq[hh * 256:(hh + 1) * 256, :].rearrange("(dt p) t -> p dt t", p=128), reads=['zq', 'qh'], writes=['qh'])
                    S.dma('sp', gh[:], zg[hh * 256:(hh + 1) * 256, :].rearrange("(dt p) t -> p dt t", p=128), reads=['zq', 'gh'], writes=['gh'])
                    ci = 0
                    for qb in range(2):
                        q0, n, nq = qb * 512, 512, 4
                        kl = [(L + 128 * m, 16 + m, -256 + 128 * m) for m in range(2)]
                        kl += [(128 * j, j, 128 * j) for j in range(16)]
                        kl += [(L + 128 * m, 16 + m, 2048 + 128 * m) for m in range(2)]
                        for idx, (kcol, vchunk, kbase) in enumerate(kl):
                            p = ci % 2
                            ci += 1
                            S.mm(pss[p][:, :n], [(kh[:, dt_, kcol:kcol + 128], qh[:, dt_, q0:q0 + n]) for dt_ in range(2)],
                                 reads=['kh', 'qh'], writes=[('pss', p)])
                            moff = q0 - kbase + 2304
                            assert 0 <= moff and moff + n <= RELW
                            tt('dve', ptT[:, p, :n], pss[p][:, :n], msk[:, moff:moff + n], ALU.mult,
                               reads=[('pss', p), 'm'], writes=[('ptT', p)])
                            instrs = []
                            for qi in range(nq):
                                instrs.append((pso[qi][:, 0:256], ptT[:, p, qi * 128:(qi + 1) * 128], vh[:, vchunk, :],
                                               idx == 0, idx == len(kl) - 1))
                            S.mmv(instrs, reads=[('ptT', p), 'vh'], writes=[('pso', q_) for q_ in range(4)])
                        defer_finish(q0, nq)
                    for m in range(2):
                        p = ci % 2
                        ci += 1
                        S.mm(pss[p][:, :LC], [(kh[:, dt_, L + 128 * m:L + 128 * (m + 1)], qh[:, dt_, HALF:NT]) for dt_ in range(2)],
                             reads=['kh', 'qh'], writes=[('pss', p)])
                        moff = 128 - 128 * m
                        tt('dve', ptT[:, p, :LC], pss[p][:, :LC], mskc[:, moff:moff + LC], ALU.mult,
                           reads=[('pss', p), 'mc'], writes=[('ptT', p)])
                        instrs = []
                        for qi in range(2):
                            instrs.append((pso[qi][:, 0:256], ptT[:, p, qi * 128:(qi + 1) * 128], vh[:, 16 + m, :], m == 0, m == 1))
                        S.mmv(instrs, reads=[('ptT', p), 'vh'], writes=[('pso', q_) for q_ in range(4)])
                    defer_finish(HALF, 2)
                    while pend:
                        pend.pop(0)()
                    S.dma('sp', ycat[1024 + hh * 256:1024 + (hh + 1) * 256, :].rearrange("(dt p) t -> p dt t", p=128), yco[:],
                          reads=[('yco', 0), ('yco', 1)], writes=['ycat_c'])
            S.barrier()

        def finish_queries(q0, nq, oc, s_, stt_, mv, rstd, on, pstr, gh, yco):
            for qi in range(nq):
                o = oc[:, s_, qi, :]
                k = qi % 2
                S.op('dve', lambda e: e.bn_stats(out=stt_[:, k, :], in_=o), reads=[('oc', s_, qi)], writes=[('stat', k)])
                S.op('dve', lambda e: e.bn_aggr(out=mv[:, k, :], in_=stt_[:, k, :]), reads=[('stat', k)], writes=[('mv', k)])
                act(rstd[:, k, :], mv[:, k, 1:2], AF.Sqrt, bias=cst[:, 4:5], reads=[('mv', k)], writes=[('rstd', k)])
                S.op('dve', lambda e: e.reciprocal(out=rstd[:, k, :], in_=rstd[:, k, :]), reads=[('rstd', k)], writes=[('rstd', k)])
                ts('dve', on[:, k, :], o, mv[:, k, 0:1], rstd[:, k, :], ALU.subtract, ALU.mult,
                   reads=[('oc', s_, qi), ('mv', k), ('rstd', k)], writes=[('on', k)])
                for dt_ in range(2):
                    S.op('pe', lambda e, dt_=dt_: e.transpose(pstr[k][:, dt_, :], on[:, k, dt_ * 128:(dt_ + 1) * 128], identb[:]),
                         reads=[('on', k)], writes=[('pstr', k)])
                    tt('dve', yco[:, dt_, q0 + qi * 128:q0 + (qi + 1) * 128], pstr[k][:, dt_, :], gh[:, dt_, q0 + qi * 128:q0 + (qi + 1) * 128],
                       ALU.mult, reads=[('pstr', k), 'gh'], writes=[('yco', dt_)])

        def stage_merge(l):
            TBL = TB3[:2] if l == DEPTH - 1 else TB3
            with ExitStack() as st:
                sb = lambda name, shape, dt: st.enter_context(nc.sbuf_tensor(uq(name), list(shape), dt))
                hoc = sb("mhoc", [128, 16, NT], BF16)
                yct = sb("myct", [128, 16, NT], BF16)
                wg = sb("mwg", [128, 2, 3, 16, 256], BF16)
                wp = sb("mwp", [128, 2, 16, 256], BF16)
                gs = sb("mgs", [128, 3, 512], F32)
                m1 = sb("mm1", [128, 3, 512], F32)
                mgo = sb("mmgo", [128, 2, 2, 512], BF16)
                psg = [st.enter_context(nc.psum_tensor(uq(f"mpg{i}"), [128, 512], F32)) for i in range(3)]
                psp = [st.enter_context(nc.psum_tensor(uq(f"mpp{i}"), [128, 512], F32)) for i in range(3)]
                for c in range(2):
                    S.dma('sp', hoc[:, c * 8:(c + 1) * 8, 0:HALF], hx[c].rearrange("(kt p) t -> p kt t", p=128), reads=['hx'], writes=['hoc'])
                S.dma('sp', hoc[:, :, HALF:NT], hctx.rearrange("(kt p) t -> p kt t", p=128), reads=['hctx'], writes=['hoc'])
                S.dma('sp', yct[:], ycat.rearrange("(kt p) t -> p kt t", p=128), reads=['ycat_a', 'ycat_b', 'ycat_c'], writes=['yct'])
                si = 0
                kr = [(0, 4), (4, 8), (8, 16)]
                def load(i):
                    if i >= 8:
                        return
                    sl, nn0 = i % 2, i * 256
                    for br in range(3):
                        cc0 = O_GATE + br * D + nn0
                        S.dma('pool', wg[:, sl, br], w_in[l, :, cc0:cc0 + 256].rearrange("(kt p) c -> p kt c", p=128),
                              writes=[('wg', sl, br)])
                    S.dma('pool', wp[:, sl, 0:4], p_a[l, :, nn0:nn0 + 256].rearrange("(kt p) c -> p kt c", p=128), writes=[('wp', sl, 0)])
                    S.dma('pool', wp[:, sl, 4:8], p_b[l, :, nn0:nn0 + 256].rearrange("(kt p) c -> p kt c", p=128), writes=[('wp', sl, 1)])
                    S.dma('pool', wp[:, sl, 8:16], p_c[l, :, nn0:nn0 + 256].rearrange("(kt p) c -> p kt c", p=128), writes=[('wp', sl, 2)])
                load(0)
                for nci in range(8):
                    slot = nci % 2
                    n0 = nci * 256
                    load(nci + 1)
                    for (t0, n, row) in TBL:
                        ss = si % 2
                        si += 1
                        for nt in range(2):
                            N = nci * 2 + nt
                            for br in range(3):
                                gcol = (O_GATE // 128) + br * 16 + N
                                S.mm(psg[br][:, :n], [(wg[:, slot, br, kt, nt * 128:(nt + 1) * 128], hoc[:, kt, t0:t0 + n]) for kt in range(16)],
                                     reads=[('wg', slot, br), 'hoc'], writes=[('psg', br)])
                                act(gs[:, br, :n], psg[br][:, :n], AF.Sigmoid, bias=pl[:, l, C_BIN + gcol:C_BIN + gcol + 1],
                                    reads=[('psg', br)], writes=[('gs', br)])
                                k0, k1 = kr[br]
                                S.mm(psp[br][:, :n], [(wp[:, slot, kt, nt * 128:(nt + 1) * 128], yct[:, kt, t0:t0 + n]) for kt in range(k0, k1)],
                                     reads=[('wp', slot, br), 'yct'], writes=[('psp', br)])
                                tt('dve', m1[:, br, :n], psp[br][:, :n], gs[:, br, :n], ALU.mult,
                                   reads=[('psp', br), ('gs', br)], writes=[('m1', br)])
                            tt('pool', m1[:, 0, :n], m1[:, 0, :n], m1[:, 1, :n], ALU.add, reads=[('m1', 0), ('m1', 1)], writes=[('m1', 0)])
                            tt('pool', mgo[:, ss, nt, :n], m1[:, 0, :n], m1[:, 2, :n], ALU.add, reads=[('m1', 0), ('m1', 2)], writes=[('mgo', ss, nt)])
                        S.dma('sp', mg[n0:n0 + 256, t0:t0 + n].rearrange("(nt p) t -> p nt t", p=128), mgo[:, ss, :, :n],
                              reads=[('mgo', ss, 0), ('mgo', ss, 1)], writes=['mg'])
            S.barrier()

        def layer_norm(r, sq, stt4, psA, psB, n, l, gcol, bcol):
            mean, msq, var, rstd = (stt4[:, i, :n] for i in range(4))
            for N in range(16):
                act(sq[:, N, :n], r[:, N, :n], AF.Square, reads=[('r', N)], writes=[('sq', N)])
            S.mm(psA[:, :n], [(onesd[:], r[:, N, :n]) for N in range(16)], reads=[('r', N) for N in range(16)], writes=['psA'])
            S.mm(psB[:, :n], [(onesd[:], sq[:, N, :n]) for N in range(16)], reads=[('sq', N) for N in range(16)], writes=['psB'])
            cp('dve', mean, psA[:, :n], reads=['psA'], writes=['mean'])
            tt('dve', msq, mean, mean, ALU.mult, reads=['mean'], writes=['msq'])
            tt('dve', var, psB[:, :n], msq, ALU.subtract, reads=['psB', 'msq'], writes=['var'])
            act(rstd, var, AF.Sqrt, bias=cst[:, 3:4], reads=['var'], writes=['rstd'])
            S.op('dve', lambda e: e.reciprocal(out=rstd, in_=rstd), reads=['rstd'], writes=['rstd'])
            for N in range(16):
                tt('dve', r[:, N, :n], r[:, N, :n], mean, ALU.subtract, reads=[('r', N), 'mean'], writes=[('r', N)])
                tt('pool', r[:, N, :n], r[:, N, :n], rstd, ALU.mult, reads=[('r', N), 'rstd'], writes=[('r', N)])
                act(r[:, N, :n], r[:, N, :n], AF.Identity, bias=pl[:, l, bcol + N:bcol + N + 1], scale=pl[:, l, gcol + N:gcol + N + 1],
                    reads=[('r', N)], writes=[('r', N)])

        def stage_wo_ln(l):
            TBL = TB3[:2] if l == DEPTH - 1 else TB3
            with ExitStack() as st:
                sb = lambda name, shape, dt: st.enter_context(nc.sbuf_tensor(uq(name), list(shape), dt))
                mgt = sb("omgt", [128, 16, NT], BF16)
                wo = sb("owo", [128, 2, 16, 512], BF16)
                r = sb("or", [128, 16, 512], F32)
                sq = sb("osq", [128, 16, 512], F32)
                xt = sb("oxt", [128, 16, 512], F32)
                stt4 = sb("ost4", [128, 4, 512], F32)
                hbo = sb("ohbo", [128, 16, 512], BF16)
                ps = [st.enter_context(nc.psum_tensor(uq(f"ops{i}"), [128, 512], F32)) for i in range(4)]
                psA = st.enter_context(nc.psum_tensor(uq("opsA"), [128, 512], F32))
                psB = st.enter_context(nc.psum_tensor(uq("opsB"), [128, 512], F32))
                S.dma('sp', mgt[:], mg.rearrange("(kt p) t -> p kt t", p=128), reads=['mg'], writes=['mgt'])
                ci = 0
                pi = 0

                def load(i):
                    if i < 4 * len(TBL):
                        c_ = i % 4
                        S.dma('pool', wo[:, i % 2], w_o[l, :, c_ * 512:(c_ + 1) * 512].rearrange("(kt p) c -> p kt c", p=128),
                              writes=[('wo', i % 2)])
                load(0)
                for (t0, n, row) in TBL:
                    S.dma('sp', xt[:, :, :n], xres[:, t0:t0 + n].rearrange("(nt p) t -> p nt t", p=128),
                          reads=['xres', 'xt'] + [('r', N) for N in range(16)], writes=['xt'])
                    for c in range(4):
                        slot = ci % 2
                        ci += 1
                        load(ci)
                        for ct in range(4):
                            N = c * 4 + ct
                            p = pi % 4
                            pi += 1
                            S.mm(ps[p][:, :n], [(wo[:, slot, kt, ct * 128:(ct + 1) * 128], mgt[:, kt, t0:t0 + n]) for kt in range(16)],
                                 reads=[('wo', slot), 'mgt'], writes=[('ops', p)])
                            act(r[:, N, :n], ps[p][:, :n], AF.Identity, bias=der[:, row * 16 + N:row * 16 + N + 1],
                                scale=mod[:, l, 32 + N, row:row + 1], reads=[('ops', p), 'hbo_dma', 'xres_dma'], writes=[('r', N)])
                            stt('dve', r[:, N, :n], xt[:, N, :n], ALPHA, r[:, N, :n], ALU.mult, ALU.add,
                                reads=['xt', ('r', N)], writes=[('r', N)])
                    layer_norm(r, sq, stt4, psA, psB, n, l, C_L1G, C_L1B)
                    for N in range(16):
                        if N % 2 == 0:
                            act(hbo[:, N, :n], r[:, N, :n], AF.Identity, bias=mod[:, l, 48 + N, row:row + 1],
                                scale=mod1[:, l, 64 + N, row:row + 1], reads=[('r', N), 'hbo_dma'], writes=[('hbo', N)])
                        else:
                            ts('dve', hbo[:, N, :n], r[:, N, :n], mod1[:, l, 64 + N, row:row + 1], mod[:, l, 48 + N, row:row + 1],
                               ALU.mult, ALU.add, reads=[('r', N), 'hbo_dma'], writes=[('hbo', N)])
                    S.dma('sp', xres[:, t0:t0 + n].rearrange("(nt p) t -> p nt t", p=128), r[:, :, :n],
                          reads=[('r', N) for N in range(16)] + ['xt'], writes=['xres_dma'])
                    S.dma('sp', h2[:, t0:t0 + n].rearrange("(nt p) t -> p nt t", p=128), hbo[:, :, :n],
                          reads=[('hbo', N) for N in range(16)], writes=['hbo_dma'])
            S.barrier()

        def stage_mlp(l):
            TBL = TB3[:2] if l == DEPTH - 1 else TB3
            with ExitStack() as st:
                sb = lambda name, shape, dt: st.enter_context(nc.sbuf_tensor(uq(name), list(shape), dt))
                acc = sb("pacc", [128, 16, NT], F32)
                h2t = sb("ph2t", [128, 16, NT], BF16)
                w1c = sb("pw1c", [128, 2, 16, 256], BF16)
                w2c = sb("pw2c", [128, 2, 2, D], BF16)
                hid = sb("phid", [128, 2, 2, NT], BF16)
                tmp1 = sb("ptmp1", [128, 2, 512], F32)
                tmp2 = sb("ptmp2", [128, 2, 512], F32)
                ps1 = [st.enter_context(nc.psum_tensor(uq(f"pp1{i}"), [128, 512], F32)) for i in range(2)]
                ps2 = [st.enter_context(nc.psum_tensor(uq(f"pp2{i}"), [128, 512], F32)) for i in range(6)]
                S.dma('sp', h2t[:], h2.rearrange("(kt p) t -> p kt t", p=128), reads=['h2'], writes=['h2t'])
                S.dma('sp', acc[:], xres.rearrange("(nt p) t -> p nt t", p=128), reads=['xres'], writes=['acc0'])
                for N in range(16):
                    for (t0, n, row) in TBL:
                        act(acc[:, N, t0:t0 + n], acc[:, N, t0:t0 + n], AF.Identity, bias=der[:, 32 + row * 16 + N:32 + row * 16 + N + 1],
                            scale=ALPHA, reads=['acc0'], writes=[('acc', N, t0)])
                i1 = 0
                i2 = 0
                def load(i):
                    if i < DFF // 256:
                        S.dma('pool', w1c[:, i % 2], w_m1[l, :, i * 256:(i + 1) * 256].rearrange("(kt p) c -> p kt c", p=128),
                              writes=[('w1c', i % 2)])
                        S.dma('pool', w2c[:, i % 2], w_m2[l, i * 256:(i + 1) * 256, :].rearrange("(kt p) c -> p kt c", p=128),
                              writes=[('w2c', i % 2)])
                load(0)
                for j in range(DFF // 256):
                    slot = j % 2
                    load(j + 1)
                    for ft in range(2):
                        bcol = C_B1 + j * 2 + ft
                        for (t0, n, row) in TBL:
                            p = i1 % 2
                            i1 += 1
                            S.mm(ps1[p][:, :n], [(w1c[:, slot, kt, ft * 128:(ft + 1) * 128], h2t[:, kt, t0:t0 + n]) for kt in range(16)],
                                 reads=[('w1c', slot), 'h2t'], writes=[('ps1', p)])
                            ts('dve', tmp1[:, p, :n], ps1[p][:, :n], pl[:, l, bcol:bcol + 1], 0.0, ALU.add, ALU.max,
                               reads=[('ps1', p)], writes=[('tmp1', p)])
                            act(hid[:, slot, ft, t0:t0 + n], tmp1[:, p, :n], AF.Square, reads=[('tmp1', p)], writes=[('hid', slot, ft, t0)])
                    for N in range(16):
                        for (t0, n, row) in TBL:
                            p = i2 % 6
                            i2 += 1
                            S.mm(ps2[p][:, :n], [(w2c[:, slot, kt, N * 128:(N + 1) * 128], hid[:, slot, kt, t0:t0 + n]) for kt in range(2)],
                                 reads=[('w2c', slot), ('hid', slot, 0, t0), ('hid', slot, 1, t0)], writes=[('ps2', p)])
                            g2 = mod[:, l, 80 + N, row:row + 1]
                            if i2 % 2 == 0:
                                stt('dve', acc[:, N, t0:t0 + n], ps2[p][:, :n], g2, acc[:, N, t0:t0 + n], ALU.mult, ALU.add,
                                    reads=[('ps2', p), ('acc', N, t0)], writes=[('acc', N, t0)])
                            else:
                                q = (i2 // 2) % 2
                                act(tmp2[:, q, :n], ps2[p][:, :n], AF.Identity, scale=g2, reads=[('ps2', p)], writes=[('tmp2', q)])
                                tt('pool', acc[:, N, t0:t0 + n], acc[:, N, t0:t0 + n], tmp2[:, q, :n], ALU.add,
                                   reads=[('tmp2', q), ('acc', N, t0)], writes=[('acc', N, t0)])
                S.dma('sp', xres.rearrange("(nt p) t -> p nt t", p=128), acc[:],
                      reads=[('acc', N, t0) for N in range(16) for (t0, _, _) in TB3], writes=['xres'])
            S.barrier()

        def stage_ln2(l, last):
            with ExitStack() as st:
                sb = lambda name, shape, dt: st.enter_context(nc.sbuf_tensor(uq(name), list(shape), dt))
                r = sb("lr", [128, 2, 16, 512], F32)
                sq = sb("lsq", [128, 16, 512], F32)
                stt4 = sb("lst4", [128, 4, 512], F32)
                psA = st.enter_context(nc.psum_tensor(uq("lpsA"), [128, 512], F32))
                psB = st.enter_context(nc.psum_tensor(uq("lpsB"), [128, 512], F32))
                for i, (t0, n, row) in enumerate(TB3):
                    if last and row == 1:
                        continue
                    rr = r[:, i % 2]
                    S.dma('sp', rr[:, :, :n], xres[:, t0:t0 + n].rearrange("(nt p) t -> p nt t", p=128),
                          reads=['xres'] + [('r', N) for N in range(16)], writes=[('r', N) for N in range(16)])
                    layer_norm(rr, sq, stt4, psA, psB, n, l, C_L2G, C_L2B)
                    dst = yout[:, t0:t0 + n] if last else xres[:, t0:t0 + n]
                    S.dma('sp', dst.rearrange("(nt p) t -> p nt t", p=128), rr[:, :, :n],
                          reads=[('r', N) for N in range(16)], writes=['xres_o'])
            S.barrier()

        stage_ada()
        if stop_after == 'ada':
            depth = 0
        stages = [("derive", stage_derive), ("h", stage_h), ("inproj_a", stage_inproj_a), ("inproj_b", stage_inproj_b),
                  ("pool", stage_pool), ("hyfront", stage_hyfront), ("hyena", stage_hyena), ("ret", stage_ret),
                  ("merge", stage_merge), ("wo_ln", stage_wo_ln), ("mlp", stage_mlp)]
        done = False
        for l in range(depth):
            for nm, fn in stages:
                fn(l)
                if stop_after == nm:
                    done = True
                    break
            if done:
                break
            stage_ln2(l, last=(l == DEPTH - 1))
        S.barrier()
    return nc


def _fm(t):
    return np.ascontiguousarray(np.asarray(t, np.float32).reshape(-1, 128).T)


_CONST_CACHE = {}


def _tables():
    if _CONST_CACHE:
        return _CONST_CACHE
    C = _CONST_CACHE
    f64 = np.float64
    for name, Lq in (("zft", L), ("zftc", LC)):
        t = np.linspace(0.0, 1.0, Lq, dtype=np.float32)[:, None]
        w = (2.0 * math.pi * np.arange(Lq, dtype=np.float32)[:, None] / Lq).astype(np.float32)
        f = np.linspace(1e-4, 15, 16, dtype=np.float32)[None, :]
        z = np.concatenate([t, np.cos(f * w), -np.sin(f * w)], axis=-1).astype(np.float32)
        C[name] = np.ascontiguousarray(z.T)
    tn = np.zeros((128, 18), np.float32)
    for tt_ in range(16):
        tn[:, tt_] = np.linspace(0.0, 1.0, L, dtype=np.float32)[tt_ * 128:(tt_ + 1) * 128]
    for j in range(2):
        tn[:, 16 + j] = np.linspace(0.0, 1.0, LC, dtype=np.float32)[j * 128:(j + 1) * 128]
    C["tn"] = tn
    max_decay = math.log(1e-2) / 0.3
    min_decay = math.log(1e-2) / 1.5
    deltas = np.linspace(min_decay, max_decay, 512, dtype=np.float32)
    C["nad"] = np.ascontiguousarray(np.broadcast_to(-np.abs(deltas)[None, :], (128, 512))).astype(np.float32)

    def dft_tables(Lq):
        N = 2 * Lq
        nf = Lq
        s = np.arange(Lq)
        f = np.arange(nf)
        ang = 2.0 * np.pi * ((np.outer(s, f)) % N).astype(f64) / N
        Cm = np.cos(ang)
        Sm = np.sin(ang)
        Sm[:, 0] = np.where(s % 2 == 0, 1.0, -1.0)
        return Cm, Sm, N

    Cm, Sm, N = dft_tables(L)
    fm = np.zeros((16, 128, 16, 256), np.float32)
    for i in range(16):
        blkc = Cm[:, i * 128:(i + 1) * 128].reshape(16, 128, 128)
        blks = Sm[:, i * 128:(i + 1) * 128].reshape(16, 128, 128)
        fm[i, :, :, 0:128] = blkc.transpose(1, 0, 2)
        fm[i, :, :, 128:256] = blks.transpose(1, 0, 2)
    C["fm"] = fm
    Cc, Sc, Nc = dft_tables(LC)
    fmc = np.zeros((2, 128, 2, 256), np.float32)
    for i in range(2):
        fmc[i, :, :, 0:128] = Cc[:, i * 128:(i + 1) * 128].reshape(2, 128, 128).transpose(1, 0, 2)
        fmc[i, :, :, 128:256] = Sc[:, i * 128:(i + 1) * 128].reshape(2, 128, 128).transpose(1, 0, 2)
    C["fmc"] = fmc

    def inv_table(Lq, tpos):
        N = 2 * Lq
        f = np.arange(Lq)
        ang = 2.0 * np.pi * ((np.outer(f, tpos)) % N).astype(f64) / N
        a = np.full((Lq, 1), 2.0 / N)
        a[0, 0] = 1.0 / N
        Gc = a * np.cos(ang)
        Gs = -(2.0 / N) * np.sin(ang)
        Gs[0, :] = (1.0 / N) * np.where(tpos % 2 == 0, 1.0, -1.0)
        return np.concatenate([Gc, Gs], axis=0).astype(np.float32)

    C["gm"] = [inv_table(L, np.arange(s * HALF, (s + 1) * HALF)).reshape(32, 128, HALF) for s in range(2)]
    C["gmc"] = inv_table(LC, np.arange(LC)).reshape(4, 128, LC)
    inv = (10000.0 ** (-np.arange(64, dtype=np.float32) / 64)).astype(np.float32)
    tpos = np.arange(L)
    row = (tpos // 64).astype(np.float32)
    col = (tpos % 64).astype(np.float32)
    ang_r = (row[None, :] * inv[:, None]).astype(np.float32)
    ang_c = (col[None, :] * inv[:, None]).astype(np.float32)
    rp = np.zeros((128, 4, L), np.float32)
    rp[:, 0] = np.tile(np.cos(ang_r), (2, 1))
    rp[:, 1] = np.tile(np.cos(ang_c), (2, 1))
    rp[:, 2] = np.tile(np.sin(ang_r), (2, 1))
    rp[:, 3] = np.tile(np.sin(ang_c), (2, 1))
    C["ropeg"] = rp
    C["ropeo"] = [np.ascontiguousarray(rp[:, :, s * HALF:(s + 1) * HALF]) for s in range(2)]
    pm = np.zeros((128, 128), np.float32)
    for m in range(64):
        pm[m + 64, m] = -1.0
        pm[m, m + 64] = 1.0
    C["pm"] = pm
    C["ident"] = np.eye(128, dtype=np.float32)
    wins = (2, 4, 8, 16)
    pinv = []
    for s in range(2):
        a = np.zeros((4, NT), np.float32)
        for g, w in enumerate(wins):
            t = np.arange(s * HALF, (s + 1) * HALF)
            lo = np.clip(t - w // 2, 0, L)
            hi = np.clip(t + w - w // 2, 0, L)
            a[g, :HALF] = 1.0 / (hi - lo)
            t = np.arange(LC)
            lo = np.clip(t - w // 2, 0, LC)
            hi = np.clip(t + w - w // 2, 0, LC)
            a[g, HALF:] = 1.0 / (hi - lo)
        pinv.append(np.ascontiguousarray(np.broadcast_to(a[None], (128, 4, NT))).astype(np.float32))
    C["pinv"] = pinv
    C["halom"] = [np.ascontiguousarray(np.broadcast_to(np.array([[0.0, 1.0]], np.float32), (128, 2))),
                  np.ascontiguousarray(np.broadcast_to(np.array([[1.0, 0.0]], np.float32), (128, 2)))]
    p = np.arange(128, dtype=np.float32)[:, None]
    u = np.arange(RELW, dtype=np.float32)[None, :]
    C["relT"] = [(u - 2304.0 + s * HALF - p).astype(np.float32) for s in range(2)]
    C["relC"] = (np.arange(384, dtype=np.float32)[None, :] - 128.0 - p).astype(np.float32)
    return C


def _layer_params(inp):
    pl = np.zeros((128, DEPTH, NPL), np.float32)
    for l in range(DEPTH):
        pl[:, l, C_BADA:C_BADA + 96] = _fm(inp["b_ada"][l])
        pl[:, l, C_BIN:C_BIN + 96] = _fm(inp["b_in"][l])
        pl[:, l, C_BO:C_BO + 16] = _fm(inp["b_o"][l])
        pl[:, l, C_L1G:C_L1G + 16] = _fm(inp["ln1_g"][l])
        pl[:, l, C_L1B:C_L1B + 16] = _fm(inp["ln1_b"][l])
        pl[:, l, C_B2:C_B2 + 16] = _fm(inp["b_mlp2"][l])
        pl[:, l, C_L2G:C_L2G + 16] = _fm(inp["ln2_g"][l])
        pl[:, l, C_L2B:C_L2B + 16] = _fm(inp["ln2_b"][l])
        pl[:, l, C_B1:C_B1 + 64] = _fm(inp["b_mlp1"][l])
        pl[:, l, C_CB:C_CB + 12] = _fm(inp["conv_b"][l])
        for k in range(3):
            pl[:, l, C_CW + 12 * k:C_CW + 12 * (k + 1)] = _fm(inp["conv_w"][l, k])
        pl[:, l, C_PS:C_PS + 4] = _fm(inp["pool_scale"][l])
        for k, (bn, fn) in enumerate((("filt_b1", "filt_f1"), ("filt_b2", "filt_f2"), ("filt_b3", "filt_f3"))):
            pl[0:64, l, C_FILT + 2 * k] = inp[bn][l]
            pl[0:64, l, C_FILT + 2 * k + 1] = inp[fn][l]
        pl[:, l, C_RET:C_RET + 8] = np.asarray(inp["ret_decay"][l], np.float32).reshape(1, 8)
    return pl


def make_in_maps(inp, depth=DEPTH):
    C = _tables()
    inp = {k: np.asarray(v) for k, v in inp.items()}
    pl = _layer_params(inp)
    bvbc = np.ascontiguousarray(np.broadcast_to(inp["b_in"][:, None, O_V:O_G], (DEPTH, 128, 1024))).astype(np.float32)
    hyd = np.ascontiguousarray(inp["hyena_d"].reshape(DEPTH, 1, 512)).astype(np.float32)
    shared = {k: np.ascontiguousarray(inp[k][:depth], dtype=np.float32) for k in
              ("w_ada", "w_in", "pool_w", "filt_w1", "filt_w2", "filt_w3", "filt_w4", "p_a", "p_b", "p_c", "w_o", "w_mlp1", "w_mlp2")}
    shared.update(pl=pl, bvbc=bvbc[:depth], hyd=hyd[:depth], zft=C["zft"], zftc=C["zftc"], tn=C["tn"], nad=C["nad"], fm=C["fm"], fmc=C["fmc"],
                  gmc=C["gmc"], ropeg=C["ropeg"], pm=C["pm"], ident=C["ident"], relC=C["relC"])
    maps = []
    for core in range(8):
        b, s = core // 2, core % 2
        m = dict(shared)
        xin = np.empty((D, NT), np.float32)
        xin[:, :HALF] = inp["x"][b, s * HALF:(s + 1) * HALF, :].T
        xin[:, HALF:] = inp["ctx"][b].T
        m["xin"] = xin
        cv = np.empty((128, 16, 2), np.float32)
        cv[:, :, 0] = _fm(inp["c"][b])
        cv[:, :, 1] = _fm(inp["c_ctx"])
        m["cvec"] = cv
        m["gm"] = C["gm"][s]
        m["ropeo"] = C["ropeo"][s]
        m["pinv"] = C["pinv"][s]
        m["halom"] = C["halom"][s]
        m["relT"] = C["relT"][s]
        maps.append(m)
    return maps


_NC_CACHE = {}


def kernel(**inputs):
    maps = make_in_maps(inputs)
    if "nc" not in _NC_CACHE:
        _NC_CACHE["nc"] = build_program()
    res = run_bass_kernel_spmd(_NC_CACHE["nc"], maps, core_ids=list(range(8)))
    out = np.empty((4, L, D), np.float32)
    for core in range(8):
        b, s = core // 2, core % 2
        out[b, s * HALF:(s + 1) * HALF, :] = res.results[core]["yout"].T
    return out
```

```python
import math
from contextlib import ExitStack
import numpy as np
import concourse.bass as bass
import concourse.mybir as mybir
from concourse.bass_utils import run_bass_kernel_spmd

F32 = mybir.dt.float32
BF16 = mybir.dt.bfloat16
AF = mybir.ActivationFunctionType
ALU = mybir.AluOpType

D = 2048
L = 2048
LC = 256
HALF = 1024
NT = HALF + LC
DEPTH = 4
DIN = 12288
O_HY, O_Q, O_K, O_V, O_G, O_GATE = 512, 2048, 3072, 4096, 5120, 6144
DFF = 8192
ALPHA = (2 * DEPTH) ** 0.25
NPL = 420
C_BADA, C_BIN, C_BO, C_L1G, C_L1B, C_B2, C_L2G, C_L2B, C_B1 = 0, 96, 192, 208, 224, 240, 256, 272, 288
C_CB, C_CW, C_PS, C_FILT, C_RET = 352, 364, 400, 404, 410
TWO_PI = 2.0 * math.pi
RELW = 3584
DEBUG_OUT = []


class Sched:
    def __init__(self, nc, stack, n_dma_sems=6):
        self.nc = nc
        self.eng = {'pe': nc.tensor, 'act': nc.scalar, 'dve': nc.vector,
                    'pool': nc.gpsimd, 'sp': nc.sync}
        self.sem = {k: stack.enter_context(nc.semaphore(f"s_{k}")) for k in self.eng}
        self.cnt = {k: 0 for k in self.eng}
        self.seen = {k: {} for k in self.eng}
        self.last_w = {}
        self.readers = {}
        self.dsem = {}
        for q in ('sp', 'pool', 'act'):
            self.dsem[q] = [[stack.enter_context(nc.semaphore(f"d_{q}{i}")), 0]
                            for i in range(4 if q == 'pool' else n_dma_sems)]
        self.dnext = {q: 0 for q in self.dsem}
        self.csem = stack.enter_context(nc.semaphore("c_sem"))
        self.ccnt = 0

    def _wait(self, ek, h):
        sem, val, src = h
        if src == ek and ek == 'pe':
            return
        sid = id(sem)
        if self.seen[ek].get(sid, 0) >= val:
            return
        self.eng[ek].wait_ge(sem, val)
        self.seen[ek][sid] = val

    def _deps(self, ek, reads, writes):
        for r in reads:
            h = self.last_w.get(r)
            if h is not None:
                self._wait(ek, h)
        for w in writes:
            h = self.last_w.get(w)
            if h is not None:
                self._wait(ek, h)
            for h in self.readers.get(w, ()):
                self._wait(ek, h)

    def _commit(self, h, reads, writes):
        for w in writes:
            self.last_w[w] = h
            self.readers[w] = []
        for r in reads:
            if r in writes:
                continue
            self.readers.setdefault(r, []).append(h)

    def op(self, ek, fn, reads=(), writes=()):
        self._deps(ek, reads, writes)
        ins = fn(self.eng[ek])
        self.cnt[ek] += 1
        ins.then_inc(self.sem[ek], 1)
        h = (self.sem[ek], self.cnt[ek], ek)
        self._commit(h, reads, writes)
        return h

    def mmv(self, instrs, reads=(), writes=()):
        self._deps('pe', reads, writes)
        ins = None
        for (o, l, r, st, sp) in instrs:
            ins = self.nc.tensor.matmul(o, l, r, start=st, stop=sp)
        self.cnt['pe'] += 1
        ins.then_inc(self.sem['pe'], 1)
        h = (self.sem['pe'], self.cnt['pe'], 'pe')
        self._commit(h, reads, writes)
        return h

    def mm(self, out_ap, pairs, reads=(), writes=()):
        n = len(pairs)
        return self.mmv([(out_ap, l, r, i == 0, i == n - 1) for i, (l, r) in enumerate(pairs)],
                        reads, writes)

    def dma(self, q, out, in_, reads=(), writes=(), **kw):
        slot = self.dsem[q][self.dnext[q]]
        self.dnext[q] = (self.dnext[q] + 1) % len(self.dsem[q])
        sem, uses = slot
        if uses:
            self._wait(q, (sem, 16 * uses, 'dma'))
        self._deps(q, reads, writes)
        self.eng[q].dma_start(out=out, in_=in_, **kw).then_inc(sem, 16)
        slot[1] += 1
        h = (sem, 16 * slot[1], 'dma')
        self._commit(h, reads, writes)
        return h

    def coll(self, fn, reads=(), writes=()):
        self._deps('pool', reads, writes)
        self.ccnt += 1
        fn(self.eng['pool']).then_inc(self.csem, 16)
        h = (self.csem, 16 * self.ccnt, 'dma')
        self._commit(h, reads, writes)
        return h

    def barrier(self):
        hs = []
        for q in self.dsem:
            for sem, uses in self.dsem[q]:
                if uses:
                    hs.append((sem, 16 * uses, 'dma'))
        for k in self.eng:
            if self.cnt[k]:
                hs.append((self.sem[k], self.cnt[k], 'x'))
        if self.ccnt:
            hs.append((self.csem, 16 * self.ccnt, 'dma'))
        for k in self.eng:
            for h in hs:
                self._wait(k, h)
        self.last_w = {}
        self.readers = {}


def build_program(depth=DEPTH, debug_out=(), stop_after=None, ncores=8):
    nc = bass.Bass("TRN2", target_bir_lowering=False)

    def din(name, shape, dt=F32):
        return nc.dram_tensor(name, list(shape), dt, kind="ExternalInput").ap()

    def dscr(name, shape, dt):
        kind = "ExternalOutput" if name in debug_out else "Internal"
        return nc.dram_tensor(name, list(shape), dt, kind=kind).ap()

    xin = din("xin", [D, NT])
    cvec = din("cvec", [128, 16, 2])
    w_ada = din("w_ada", [depth, D, 6 * D])
    w_in = din("w_in", [depth, D, DIN])
    pool_w = din("pool_w", [depth, 4, 128, 128])
    fw1 = din("filt_w1", [depth, 33, 64])
    fw2 = din("filt_w2", [depth, 64, 64])
    fw3 = din("filt_w3", [depth, 64, 64])
    fw4 = din("filt_w4", [depth, 64, 1024])
    p_a = din("p_a", [depth, 512, D])
    p_b = din("p_b", [depth, 512, D])
    p_c = din("p_c", [depth, 1024, D])
    w_o = din("w_o", [depth, D, D])
    w_m1 = din("w_mlp1", [depth, D, DFF])
    w_m2 = din("w_mlp2", [depth, DFF, D])
    plin = din("pl", [128, DEPTH, NPL])
    bvbc = din("bvbc", [depth, 128, 1024])
    hyd = din("hyd", [depth, 1, 512])
    zft = din("zft", [33, L])
    zftc = din("zftc", [33, LC])
    tnin = din("tn", [128, 18])
    nadin = din("nad", [128, 512])
    fm = din("fm", [16, 128, 16, 256])
    fmc = din("fmc", [2, 128, 2, 256])
    gm = din("gm", [32, 128, HALF])
    gmc = din("gmc", [4, 128, LC])
    ropeg = din("ropeg", [128, 4, L])
    ropeo = din("ropeo", [128, 4, HALF])
    pmin = din("pm", [128, 128])
    identin = din("ident", [128, 128])
    pinv = din("pinv", [128, 4, NT])
    halom = din("halom", [128, 2])
    relT = din("relT", [128, RELW])
    relC = din("relC", [128, 384])
    yout = nc.dram_tensor("yout", [D, HALF], F32, kind="ExternalOutput").ap()

    xres = dscr("xres", [D, NT], F32)
    hx = [dscr(f"hx{c}", [D // 2, HALF], BF16) for c in range(2)]
    hfull = [dscr(f"hfull{c}", [D, HALF], BF16) for c in range(2)]
    hctx = dscr("hctx", [D, LC], BF16)
    z1 = dscr("z1", [2, D, HALF], F32)
    z1c = dscr("z1c", [D, LC], F32)
    zk = dscr("zk", [1024, L + LC], BF16)
    zv = dscr("zv", [L + LC, 1024], BF16)
    zq = dscr("zq", [1024, NT], BF16)
    zg = dscr("zg", [1024, NT], BF16)
    ycat = dscr("ycat", [D, NT], BF16)
    uut = dscr("uut", [L + LC, 512], BF16)
    x0c = dscr("x0c", [512, NT], F32)
    mg = dscr("mg", [D, NT], BF16)
    h2 = dscr("h2", [D, NT], BF16)

    _uc = [0]

    def uq(name):
        _uc[0] += 1
        return f"{name}_{_uc[0]}"

    with ExitStack() as st0:
        S = Sched(nc, st0)
        sb0 = lambda name, shape, dt: st0.enter_context(nc.sbuf_tensor(uq(name), list(shape), dt))
        cst = sb0("cst", [128, 8], F32)
        mod = sb0("mod", [128, DEPTH, 96, 2], F32)
        mod1 = sb0("mod1", [128, DEPTH, 96, 2], F32)
        pl = sb0("plt", [128, DEPTH, NPL], F32)
        identb = sb0("identb", [128, 128], BF16)
        pm = sb0("pmt", [128, 128], F32)
        onesd = sb0("onesd", [128, 128], F32)
        der = sb0("der", [128, 64], F32)
        hm = sb0("hm", [128, 2], F32)
        ZERO = cst[:, 0:1]
        NEGPI = cst[:, 1:2]

        S.op('dve', lambda e: e.memset(cst[:, 0:1], 0.0), writes=['cst'])
        S.op('dve', lambda e: e.memset(cst[:, 1:2], -math.pi), writes=['cst'])
        S.op('dve', lambda e: e.memset(cst[:, 2:3], 1.0), writes=['cst'])
        S.op('dve', lambda e: e.memset(cst[:, 3:4], 1e-5), writes=['cst'])
        S.op('dve', lambda e: e.memset(cst[:, 4:5], 1e-6), writes=['cst'])
        S.op('dve', lambda e: e.memset(onesd[:], 1.0 / D), writes=['onesd'])
        S.dma('sp', pl[:], plin, writes=['pl'])
        S.dma('sp', pm[:], pmin, writes=['pm'])
        S.dma('sp', hm[:], halom, writes=['hm'])
        S.dma('pool', identb[:], identin, writes=['identb'])
        S.dma('sp', xres, xin, writes=['xres'])
        S.barrier()

        def act(out, in_, func, bias=None, scale=1.0, reads=(), writes=()):
            b = ZERO[0:in_.shape[0], :] if bias is None else bias
            return S.op('act', lambda e: e.activation(out=out, in_=in_, func=func, bias=b, scale=scale),
                        reads, writes)

        def tt(ek, out, a, b, op, reads=(), writes=()):
            return S.op(ek, lambda e: e.tensor_tensor(out=out, in0=a, in1=b, op=op), reads, writes)

        def ts(ek, out, a, s1, s2, op0, op1=None, reads=(), writes=()):
            if op1 is None:
                return S.op(ek, lambda e: e.tensor_scalar(out=out, in0=a, scalar1=s1, scalar2=None, op0=op0),
                            reads, writes)
            return S.op(ek, lambda e: e.tensor_scalar(out=out, in0=a, scalar1=s1, scalar2=s2, op0=op0, op1=op1),
                        reads, writes)

        def stt(ek, out, a, s, b, op0, op1, reads=(), writes=()):
            return S.op(ek, lambda e: e.scalar_tensor_tensor(out=out, in0=a, scalar=s, in1=b, op0=op0, op1=op1),
                        reads, writes)

        def cp(ek, out, a, reads=(), writes=()):
            if ek == 'act':
                return act(out, a, AF.Identity, reads=reads, writes=writes)
            return S.op(ek, lambda e: e.tensor_copy(out=out, in_=a), reads, writes)

        TB3 = [(0, 512, 0), (512, 512, 0), (1024, 256, 1)]

        def stage_ada():
            with ExitStack() as st:
                sb = lambda name, shape, dt: st.enter_context(nc.sbuf_tensor(uq(name), list(shape), dt))
                wada = sb("wada", [128, 2, 16, 512], BF16)
                cv = sb("cv", [128, 16, 2], F32)
                scv = sb("scv", [128, 16, 2], BF16)
                ps = st.enter_context(nc.psum_tensor(uq("psada"), [128, 96, 2], F32))
                S.dma('sp', cv[:], cvec, writes=['cv'])
                act(scv[:], cv[:], AF.Silu, reads=['cv'], writes=['scv'])
                def load(i):
                    if i >= depth * 24:
                        return
                    l_, c_ = divmod(i, 24)
                    S.dma('pool', wada[:, i % 2], w_ada[l_, :, c_ * 512:(c_ + 1) * 512].rearrange("(kt p) c -> p kt c", p=128),
                          writes=[('wada', i % 2)])
                ci = 0
                load(0)
                for l in range(depth):
                    for c in range(24):
                        slot = ci % 2
                        ci += 1
                        load(ci)
                        for ct in range(4):
                            T = c * 4 + ct
                            S.mm(ps[:, T, :], [(wada[:, slot, kt, ct * 128:(ct + 1) * 128], scv[:, kt, :]) for kt in range(16)],
                                 reads=[('wada', slot), 'scv'], writes=['psada'])
                    for r in range(2):
                        tt('dve', mod[:, l, :, r], ps[:, :, r], pl[:, l, C_BADA:C_BADA + 96], ALU.add,
                           reads=['psada', 'pl'], writes=[('mod', l, r)])
                        ts('dve', mod1[:, l, :, r], mod[:, l, :, r], 1.0, None, ALU.add,
                           reads=[('mod', l, r)], writes=[('mod1', l, r)])
            S.barrier()

        def stage_h(l):
            with ExitStack() as st:
                sb = lambda name, shape, dt: st.enter_context(nc.sbuf_tensor(uq(name), list(shape), dt))
                xb = sb("xb", [128, 2, 16, 512], F32)
                hb = sb("hb", [128, 2, 16, 512], BF16)
                for i, (t0, n, row) in enumerate(TB3):
                    slot = i % 2
                    S.dma('sp', xb[:, slot, :, :n], xres[:, t0:t0 + n].rearrange("(nt p) t -> p nt t", p=128),
                          reads=['xres'], writes=[('xb', slot)])
                    for nt in range(16):
                        if nt % 2 == 0:
                            act(hb[:, slot, nt, :n], xb[:, slot, nt, :n], AF.Identity, bias=mod[:, l, nt, row:row + 1],
                                scale=mod1[:, l, 16 + nt, row:row + 1], reads=[('xb', slot)], writes=[('hb', slot, nt)])
                        else:
                            ts('dve', hb[:, slot, nt, :n], xb[:, slot, nt, :n], mod1[:, l, 16 + nt, row:row + 1],
                               mod[:, l, nt, row:row + 1], ALU.mult, ALU.add, reads=[('xb', slot)], writes=[('hb', slot, nt)])
                    rk = [('hb', slot, nt) for nt in range(16)]
                    if row == 0:
                        for c in range(2):
                            S.dma('sp', hx[c][:, t0:t0 + n].rearrange("(nt p) t -> p nt t", p=128), hb[:, slot, c * 8:(c + 1) * 8, :n],
                                  reads=rk, writes=['hx'])
                    else:
                        S.dma('sp', hctx.rearrange("(nt p) t -> p nt t", p=128), hb[:, slot, :, :n], reads=rk, writes=['hctx'])
                groups = [[2 * i, 2 * i + 1] for i in range(ncores // 2)]
                for c in range(2):
                    S.op('pool', lambda e, c=c: e.collective_compute("AllGather", ALU.bypass, replica_groups=groups,
                                                                      ins=[hx[c]], outs=[hfull[c]]),
                         reads=['hx'], writes=['hfull'])
            S.barrier()

        def stage_derive(l):
            for r in range(2):
                tt('dve', der[:, r * 16:(r + 1) * 16], mod[:, l, 32:48, r], pl[:, l, C_BO:C_BO + 16], ALU.mult, writes=[('der', r)])
                tt('dve', der[:, 32 + r * 16:32 + (r + 1) * 16], mod[:, l, 80:96, r], pl[:, l, C_B2:C_B2 + 16], ALU.mult,
                   writes=[('der', 2 + r)])
            S.barrier()

        def stage_inproj_a(l):
            with ExitStack() as st:
                sb = lambda name, shape, dt: st.enter_context(nc.sbuf_tensor(uq(name), list(shape), dt))
                hall = sb("hall", [128, 16, L + LC], BF16)
                wch = sb("wch", [128, 2, 16, 512], BF16)
                rope = sb("rope", [128, 4, L], F32)
                zst = sb("zst", [128, 2, 4, 512], F32)
                kf = sb("kf", [128, 2, 512], F32)
                t1 = sb("t1", [128, 2, 512], F32)
                ko = sb("ko", [128, 2, 4, 512], BF16)
                vo = sb("vo", [128, 2, 512], BF16)
                bvb = sb("bvb", [128, 1024], F32)
                bk16 = sb("bk16", [128, 8], F32)
                psm = [st.enter_context(nc.psum_tensor(uq(f"psm{i}"), [128, 512], F32)) for i in range(4)]
                psr = [st.enter_context(nc.psum_tensor(uq(f"psr{i}"), [128, 512], F32)) for i in range(2)]
                for r in range(2):
                    for c in range(2):
                        S.dma('sp', hall[:, c * 8:(c + 1) * 8, r * HALF:(r + 1) * HALF],
                              hfull[c][r * (D // 2):(r + 1) * (D // 2), :].rearrange("(kt p) t -> p kt t", p=128),
                              reads=['hfull'], writes=['hall'])
                S.dma('sp', hall[:, :, L:L + LC], hctx.rearrange("(kt p) t -> p kt t", p=128), reads=['hctx'], writes=['hall'])
                S.dma('sp', rope[:], ropeg, writes=['rope'])
                S.dma('sp', bvb[:], bvbc[l], writes=['bvb'])
                ts('dve', bk16[:], pl[:, l, C_BIN + 24:C_BIN + 32], 1.0 / 16.0, None, ALU.mult, writes=['bk16'])
                TBA = [(0, 512), (512, 512), (1024, 512), (1536, 512), (2048, 256)]
                chunks = [('z', 0), ('z', 512), ('z', 1024), ('z', 1536), ('k', O_K), ('k', O_K + 512), ('v', O_V), ('v', O_V + 512)]
                pi = 0
                si = 0
                def load(i):
                    if i < len(chunks):
                        cc0 = chunks[i][1]
                        S.dma('pool', wch[:, i % 2], w_in[l, :, cc0:cc0 + 512].rearrange("(kt p) c -> p kt c", p=128),
                              writes=[('wch', i % 2)])
                load(0)
                for ci, (kind, c0) in enumerate(chunks):
                    slot = ci % 2
                    load(ci + 1)
                    if kind == 'v':
                        cv0 = c0 - O_V
                        for tti in range(18):
                            p = pi % 4
                            pi += 1
                            S.mm(psm[p][:], [(hall[:, kt, tti * 128:(tti + 1) * 128], wch[:, slot, kt, :]) for kt in range(16)],
                                 reads=[('wch', slot), 'hall'], writes=[('psm', p)])
                            vs = tti % 2
                            tt('dve', vo[:, vs, :], psm[p][:], bvb[:, cv0:cv0 + 512], ALU.add,
                               reads=[('psm', p), 'bvb'], writes=[('vo', vs)])
                            S.dma('sp', zv[tti * 128:(tti + 1) * 128, cv0:cv0 + 512], vo[:, vs, :], reads=[('vo', vs)], writes=['zv'])
                        continue
                    for (t0, n) in TBA:
                        ss = si % 2
                        si += 1
                        for ct in range(4):
                            p = pi % 4
                            pi += 1
                            gcol = (c0 // 128) + ct
                            S.mm(psm[p][:, :n], [(wch[:, slot, kt, ct * 128:(ct + 1) * 128], hall[:, kt, t0:t0 + n]) for kt in range(16)],
                                 reads=[('wch', slot), 'hall'], writes=[('psm', p)])
                            if kind == 'z':
                                act(zst[:, ss, ct, :n], psm[p][:, :n], AF.Identity, bias=pl[:, l, C_BIN + gcol:C_BIN + gcol + 1],
                                    reads=[('psm', p)], writes=[('zst', ss, ct)])
                            else:
                                kt8 = gcol - 24
                                fs = (si + ct) % 2
                                act(kf[:, fs, :n], psm[p][:, :n], AF.Identity, bias=bk16[:, kt8:kt8 + 1], scale=1.0 / 16.0,
                                    reads=[('psm', p), 'bk16'], writes=[('kf', fs)])
                                if t0 < L:
                                    var = kt8 % 2
                                    S.mm(psr[fs][:, :n], [(pm[:], kf[:, fs, :n])], reads=[('kf', fs)], writes=[('psr', fs)])
                                    tt('dve', t1[:, fs, :n], kf[:, fs, :n], rope[:, var, t0:t0 + n], ALU.mult,
                                       reads=[('kf', fs), 'rope'], writes=[('t1', fs)])
                                    tt('dve', kf[:, fs, :n], psr[fs][:, :n], rope[:, 2 + var, t0:t0 + n], ALU.mult,
                                       reads=[('psr', fs), 'rope'], writes=[('kf', fs)])
                                    tt('pool', ko[:, ss, ct, :n], t1[:, fs, :n], kf[:, fs, :n], ALU.add,
                                       reads=[('t1', fs), ('kf', fs)], writes=[('ko', ss, ct)])
                                else:
                                    cp('pool', ko[:, ss, ct, :n], kf[:, fs, :n], reads=[('kf', fs)], writes=[('ko', ss, ct)])
                        if kind == 'z':
                            if t0 < L:
                                dst = z1[t0 // HALF, c0:c0 + 512, (t0 % HALF):(t0 % HALF) + n]
                            else:
                                dst = z1c[c0:c0 + 512, :]
                            S.dma('sp', dst.rearrange("(ct p) t -> p ct t", p=128), zst[:, ss, :, :n],
                                  reads=[('zst', ss, ct) for ct in range(4)], writes=['z1'])
                        else:
                            r0 = c0 - O_K
                            S.dma('sp', zk[r0:r0 + 512, t0:t0 + n].rearrange("(ct p) t -> p ct t", p=128), ko[:, ss, :, :n],
                                  reads=[('ko', ss, ct) for ct in range(4)], writes=['zk'])
            S.barrier()

        def stage_inproj_b(l):
            with ExitStack() as st:
                sb = lambda name, shape, dt: st.enter_context(nc.sbuf_tensor(uq(name), list(shape), dt))
                hoc = sb("hoc", [128, 16, NT], BF16)
                wch = sb("wch", [128, 2, 16, 512], BF16)
                rope = sb("rope", [128, 4, HALF], F32)
                kf = sb("kf", [128, 2, 512], F32)
                t1 = sb("t1", [128, 2, 512], F32)
                ko = sb("ko", [128, 2, 4, 512], BF16)
                psm = [st.enter_context(nc.psum_tensor(uq(f"psm{i}"), [128, 512], F32)) for i in range(4)]
                psr = [st.enter_context(nc.psum_tensor(uq(f"psr{i}"), [128, 512], F32)) for i in range(2)]
                for c in range(2):
                    S.dma('sp', hoc[:, c * 8:(c + 1) * 8, 0:HALF], hx[c].rearrange("(kt p) t -> p kt t", p=128), reads=['hx'], writes=['hoc'])
                S.dma('sp', hoc[:, :, HALF:NT], hctx.rearrange("(kt p) t -> p kt t", p=128), reads=['hctx'], writes=['hoc'])
                S.dma('sp', rope[:], ropeo, writes=['rope'])
                chunks = [('q', O_Q), ('q', O_Q + 512), ('g', O_G), ('g', O_G + 512)]
                pi = 0
                si = 0
                def load(i):
                    if i < len(chunks):
                        cc0 = chunks[i][1]
                        S.dma('pool', wch[:, i % 2], w_in[l, :, cc0:cc0 + 512].rearrange("(kt p) c -> p kt c", p=128),
                              writes=[('wch', i % 2)])
                load(0)
                for ci, (kind, c0) in enumerate(chunks):
                    slot = ci % 2
                    load(ci + 1)
                    for (t0, n, row) in TB3:
                        ss = si % 2
                        si += 1
                        for ct in range(4):
                            p = pi % 4
                            pi += 1
                            gcol = (c0 // 128) + ct
                            bias = pl[:, l, C_BIN + gcol:C_BIN + gcol + 1]
                            S.mm(psm[p][:, :n], [(wch[:, slot, kt, ct * 128:(ct + 1) * 128], hoc[:, kt, t0:t0 + n]) for kt in range(16)],
                                 reads=[('wch', slot), 'hoc'], writes=[('psm', p)])
                            if kind == 'g':
                                act(ko[:, ss, ct, :n], psm[p][:, :n], AF.Silu, bias=bias, reads=[('psm', p)], writes=[('ko', ss, ct)])
                                continue
                            fs = (si + ct) % 2
                            act(kf[:, fs, :n], psm[p][:, :n], AF.Identity, bias=bias, reads=[('psm', p)], writes=[('kf', fs)])
                            if row == 0:
                                var = gcol % 2
                                S.mm(psr[fs][:, :n], [(pm[:], kf[:, fs, :n])], reads=[('kf', fs)], writes=[('psr', fs)])
                                tt('dve', t1[:, fs, :n], kf[:, fs, :n], rope[:, var, t0:t0 + n], ALU.mult,
                                   reads=[('kf', fs), 'rope'], writes=[('t1', fs)])
                                tt('dve', kf[:, fs, :n], psr[fs][:, :n], rope[:, 2 + var, t0:t0 + n], ALU.mult,
                                   reads=[('psr', fs), 'rope'], writes=[('kf', fs)])
                                tt('pool', ko[:, ss, ct, :n], t1[:, fs, :n], kf[:, fs, :n], ALU.add,
                                   reads=[('t1', fs), ('kf', fs)], writes=[('ko', ss, ct)])
                            else:
                                cp('pool', ko[:, ss, ct, :n], kf[:, fs, :n], reads=[('kf', fs)], writes=[('ko', ss, ct)])
                        dstt = zq if kind == 'q' else zg
                        r0 = c0 - (O_Q if kind == 'q' else O_G)
                        S.dma('sp', dstt[r0:r0 + 512, t0:t0 + n].rearrange("(ct p) t -> p ct t", p=128), ko[:, ss, :, :n],
                              reads=[('ko', ss, ct) for ct in range(4)], writes=['zq'])
            S.barrier()

        pid_sp = nc.sync.partition_id()
        s_own = pid_sp % 2
        s_par = (pid_sp + 1) % 2

        def stage_pool(l):
            with ExitStack() as st:
                sb = lambda name, shape, dt: st.enter_context(nc.sbuf_tensor(uq(name), list(shape), dt))
                W = HALF + 16
                WC = LC + 16
                u = sb("pu", [128, 4, W], F32)
                uc = sb("puc", [128, 4, WC], F32)
                ta = sb("pta", [128, 4, W], F32)
                tb_ = sb("ptb", [128, 4, W], F32)
                pw = sb("ppw", [128, 4, 128], F32)
                pwb = sb("ppwb", [128, 4, 128], BF16)
                pin = sb("ppin", [128, 4, NT], F32)
                pooled = sb("ppooled", [128, 4, NT], BF16)
                yo = sb("pyo", [128, 4, NT], BF16)
                ps = [st.enter_context(nc.psum_tensor(uq(f"pps{i}"), [128, 512], F32)) for i in range(4)]
                S.dma('sp', pw[:], pool_w[l].rearrange("g c d -> c g d"), writes=['pw'])
                cp('dve', pwb[:], pw[:], reads=['pw'], writes=['pwb'])
                S.dma('sp', pin[:], pinv, writes=['pin'])
                S.op('pool', lambda e: e.memset(uc[:], 0.0), writes=['uc'])
                S.dma('sp', u[:, :, 8:8 + HALF], z1[bass.ts(s_own, 1), 0:512, :].rearrange("o (g p) t -> p (o g) t", p=128),
                      reads=['z1'], writes=['u'])
                S.dma('sp', u[:, :, 0:8], z1[bass.ts(s_par, 1), 0:512, HALF - 8:HALF].rearrange("o (g p) t -> p (o g) t", p=128),
                      reads=['z1'], writes=['uL'])
                S.dma('sp', u[:, :, 8 + HALF:W], z1[bass.ts(s_par, 1), 0:512, 0:8].rearrange("o (g p) t -> p (o g) t", p=128),
                      reads=['z1'], writes=['uR'])
                ts('dve', u[:, :, 0:8], u[:, :, 0:8], hm[:, 0:1], None, ALU.mult, reads=['uL'], writes=['uL'])
                ts('dve', u[:, :, 8 + HALF:W], u[:, :, 8 + HALF:W], hm[:, 1:2], None, ALU.mult, reads=['uR'], writes=['uR'])
                S.dma('sp', uc[:, :, 8:8 + LC], z1c[0:512, :].rearrange("(g p) t -> p g t", p=128), reads=['z1', 'uc'], writes=['uc'])
                for (src, Wd, n, o0, key) in ((u, W, HALF, 0, 'u'), (uc, WC, LC, HALF, 'uc')):
                    rd = ['u', 'uL', 'uR'] if key == 'u' else ['uc']
                    for g in range(4):
                        eng = 'dve' if g % 2 == 0 else 'pool'
                        cur, curk = src, None
                        bufs = [ta, tb_]
                        steps = [(1, 0), (1, -1), (2, -2), (4, -4)][:g + 1]
                        offs = [(1, 0), (1, 1), (2, 2), (4, 4)][:g + 1]
                        lo, hi = 0, Wd
                        for k, (am, bp) in enumerate(offs):
                            nb = bufs[k % 2]
                            nlo, nhi = lo + am, hi - bp
                            tt(eng, nb[:, g, nlo:nhi], cur[:, g, nlo - am:nhi - am], cur[:, g, nlo + bp:nhi + bp], ALU.add,
                               reads=rd + [('pt', g)], writes=[('pt', g)])
                            cur = nb
                            lo, hi = nlo, nhi
                        assert lo <= 8 and hi >= 8 + n
                        ob = bufs[(g + 1) % 2]
                        tt(eng, ob[:, g, 8:8 + n], cur[:, g, 8:8 + n], pin[:, g, o0:o0 + n], ALU.mult,
                           reads=[('pt', g), 'pin'], writes=[('pt', g)])
                        tt(eng, pooled[:, g, o0:o0 + n], ob[:, g, 8:8 + n], src[:, g, 8:8 + n], ALU.subtract,
                           reads=rd + [('pt', g)], writes=[('pooled', g, o0)])
                pi = 0
                for g in range(4):
                    for (t0, n, row) in TB3:
                        p = pi % 4
                        pi += 1
                        S.mm(ps[p][:, :n], [(pwb[:, g, :], pooled[:, g, t0:t0 + n])],
                             reads=['pwb', ('pooled', g, 0), ('pooled', g, HALF)], writes=[('pps', p)])
                        act(yo[:, g, t0:t0 + n], ps[p][:, :n], AF.Identity, scale=pl[:, l, C_PS + g:C_PS + g + 1],
                            reads=[('pps', p)], writes=[('yo', g)])
                S.dma('sp', ycat[0:512, :].rearrange("(g p) t -> p g t", p=128), yo[:],
                      reads=[('yo', g) for g in range(4)], writes=['ycat_a'])
            S.barrier()

        def stage_hyfront(l):
            with ExitStack() as st:
                sb = lambda name, shape, dt: st.enter_context(nc.sbuf_tensor(uq(name), list(shape), dt))
                WL = L + 2
                WC = LC + 2
                zin = sb("hzin", [128, 2, WL], F32)
                zc = sb("hzc", [128, 2, WL], F32)
                uu = sb("huu", [128, WL], BF16)
                uts = sb("huts", [128, 18, 512], BF16)
                x0i = sb("hx0i", [128, HALF + 2], F32)
                x0ci = sb("hx0ci", [128, WC], F32)
                x0o = sb("hx0o", [128, 4, NT], F32)
                pst = [st.enter_context(nc.psum_tensor(uq(f"hpt{i}"), [128, 512], BF16)) for i in range(2)]

                def conv(eng, out, src, n, ch):
                    w = lambda k: pl[:, l, C_CW + k * 12 + ch:C_CW + k * 12 + ch + 1]
                    b = pl[:, l, C_CB + ch:C_CB + ch + 1]
                    return [(out, src, w, b, n)]

                def do_conv(out, src, n, ch, rkeys, wkey):
                    w = lambda k: pl[:, l, C_CW + k * 12 + ch:C_CW + k * 12 + ch + 1]
                    b = pl[:, l, C_CB + ch:C_CB + ch + 1]
                    act(out[:, 0:n], src[:, 1:n + 1], AF.Identity, bias=b, scale=w(1), reads=rkeys, writes=[wkey])
                    stt('dve', out[:, 0:n], src[:, 0:n], w(0), out[:, 0:n], ALU.mult, ALU.add, reads=rkeys + [wkey], writes=[wkey])
                    stt('dve', out[:, 0:n], src[:, 2:n + 2], w(2), out[:, 0:n], ALU.mult, ALU.add, reads=rkeys + [wkey], writes=[wkey])

                for ct in range(4):
                    for j, hyt in enumerate((ct, 8 + ct)):
                        r0 = O_HY + hyt * 128
                        S.op('pool', lambda e, j=j: e.memset(zin[:, j, 0:1], 0.0), writes=[('zin', j)])
                        S.op('pool', lambda e, j=j: e.memset(zin[:, j, WL - 1:WL], 0.0), reads=[('zin', j)], writes=[('zin', j)])
                        for r in range(2):
                            S.dma('sp', zin[:, j, 1 + r * HALF:1 + (r + 1) * HALF], z1[r, r0:r0 + 128, :], reads=['z1', ('zin', j)],
                                  writes=[('zin', j)])
                        do_conv(zc[:, j, :], zin[:, j, :], L, hyt, [('zin', j)], ('zc', j))
                    tt('pool', uu[:, 0:L], zc[:, 0, 0:L], zc[:, 1, 0:L], ALU.mult, reads=[('zc', 0), ('zc', 1)], writes=['uu'])
                    for tti in range(16):
                        S.op('pe', lambda e, tti=tti: e.transpose(pst[tti % 2][:, ct * 128:(ct + 1) * 128], uu[:, tti * 128:(tti + 1) * 128], identb[:]),
                             reads=['uu', 'identb'], writes=[('pst', tti % 2)])
                        cp('act' if tti % 2 else 'dve', uts[:, tti, ct * 128:(ct + 1) * 128], pst[tti % 2][:, ct * 128:(ct + 1) * 128],
                           reads=[('pst', tti % 2)], writes=[('uts', tti, ct)])
                    for j, hyt in enumerate((ct, 8 + ct)):
                        r0 = O_HY + hyt * 128
                        S.op('pool', lambda e, j=j: e.memset(zin[:, j, 0:1], 0.0), reads=[('zin', j), ('zc', j)], writes=[('zin', j)])
                        S.op('pool', lambda e, j=j: e.memset(zin[:, j, WC - 1:WC], 0.0), reads=[('zin', j)], writes=[('zin', j)])
                        S.dma('sp', zin[:, j, 1:1 + LC], z1c[r0:r0 + 128, :], reads=['z1', ('zin', j)], writes=[('zin', j)])
                        do_conv(zc[:, j, :], zin[:, j, :], LC, hyt, [('zin', j)], ('zc', j))
                    tt('pool', uu[:, 0:LC], zc[:, 0, 0:LC], zc[:, 1, 0:LC], ALU.mult, reads=[('zc', 0), ('zc', 1), 'uu'], writes=['uu'])
                    for tti in range(2):
                        S.op('pe', lambda e, tti=tti: e.transpose(pst[tti % 2][:, ct * 128:(ct + 1) * 128], uu[:, tti * 128:(tti + 1) * 128], identb[:]),
                             reads=['uu', 'identb'], writes=[('pst', tti % 2)])
                        cp('act' if tti % 2 else 'dve', uts[:, 16 + tti, ct * 128:(ct + 1) * 128], pst[tti % 2][:, ct * 128:(ct + 1) * 128],
                           reads=[('pst', tti % 2)], writes=[('uts', 16 + tti, ct)])
                    hyt = 4 + ct
                    r0 = O_HY + hyt * 128
                    S.dma('sp', x0i[:, 1:1 + HALF], z1[bass.ts(s_own, 1), r0:r0 + 128, :].rearrange("o p t -> p (o t)"),
                          reads=['z1', 'x0c_i'], writes=['x0i'])
                    S.dma('sp', x0i[:, 0:1], z1[bass.ts(s_par, 1), r0:r0 + 128, HALF - 1:HALF].rearrange("o p t -> p (o t)"),
                          reads=['z1', 'x0c_i'], writes=['x0L'], allow_slow_non_contiguous=True)
                    S.dma('sp', x0i[:, HALF + 1:HALF + 2], z1[bass.ts(s_par, 1), r0:r0 + 128, 0:1].rearrange("o p t -> p (o t)"),
                          reads=['z1', 'x0c_i'], writes=['x0R'], allow_slow_non_contiguous=True)
                    ts('dve', x0i[:, 0:1], x0i[:, 0:1], hm[:, 0:1], None, ALU.mult, reads=['x0L'], writes=['x0L'])
                    ts('dve', x0i[:, HALF + 1:HALF + 2], x0i[:, HALF + 1:HALF + 2], hm[:, 1:2], None, ALU.mult, reads=['x0R'], writes=['x0R'])
                    do_conv(x0o[:, ct, 0:HALF], x0i[:], HALF, hyt, ['x0i', 'x0L', 'x0R'], ('x0o', ct))
                    S.op('pool', lambda e: e.memset(x0ci[:, 0:1], 0.0), reads=['x0ci'], writes=['x0ci'])
                    S.op('pool', lambda e: e.memset(x0ci[:, WC - 1:WC], 0.0), reads=['x0ci'], writes=['x0ci'])
                    S.dma('sp', x0ci[:, 1:1 + LC], z1c[r0:r0 + 128, :], reads=['z1', 'x0ci'], writes=['x0ci'])
                    do_conv(x0o[:, ct, HALF:NT], x0ci[:], LC, hyt, ['x0ci'], ('x0o', ct))
                    S.op('pool', lambda e: e.memset(x0i[:, 0:1], 0.0), reads=[('x0o', ct), 'x0i', 'x0L', 'x0R'], writes=['x0c_i'])
                S.dma('sp', uut.rearrange("(tt p) c -> p tt c", p=128), uts[:],
                      reads=[('uts', t, c) for t in range(18) for c in range(4)], writes=['uut'])
                S.dma('sp', x0c.rearrange("(ct p) t -> p ct t", p=128), x0o[:], reads=[('x0o', c) for c in range(4)], writes=['x0c'])
            S.barrier()

        def stage_hyena(l):
            with ExitStack() as st:
                sb = lambda name, shape, dt: st.enter_context(nc.sbuf_tensor(uq(name), list(shape), dt))
                w1 = sb("fw1", [33, 64], F32)
                w2 = sb("fw2", [64, 64], F32)
                w3 = sb("fw3", [64, 64], F32)
                w4 = sb("fw4", [64, 1024], F32)
                zf = sb("fzf", [33, L], F32)
                ha = sb("fha", [64, L], F32)
                hb_ = sb("fhb", [64, L], F32)
                tmp = sb("ftmp", [64, 2, 512], F32)
                tmpm = sb("ftmpm", [64, 1, 512], F32)
                fb = sb("ffb", [64, 3], F32)
                tn = sb("ftn", [128, 18], F32)
                nad = sb("fnad", [128, 512], F32)
                hdr = sb("fhdr", [1, 512], F32)
                dec = sb("fdec", [128, 1, 512], F32)
                hfb = sb("fhfb", [128, 1, 2, 512], F32)
                hs = sb("fhs", [128, 16, 512], BF16)
                hd = sb("fhd", [128, 16, 512], BF16)
                uts = sb("futs", [128, 16, 512], BF16)
                fch = sb("ffch", [128, 2, 16, 256], BF16)
                ahs = sb("fahs", [128, 2, 512], F32)
                pn = sb("fpn", [1, 512], F32)
                pt = sb("fpt", [128, 4, 512], F32)
                ypk = sb("fypk", [128, 32, 512], BF16)
                gch = sb("fgch", [128, 2, 2, HALF], BF16)
                x0t = sb("fx0t", [128, 4, NT], F32)
                ybo = sb("fybo", [128, 4, NT], BF16)
                ps = [st.enter_context(nc.psum_tensor(uq(f"fps{i}"), [128, 512], F32)) for i in range(8)]

                S.dma('sp', w1[:], fw1[l], writes=['w1'])
                S.dma('sp', w2[:], fw2[l], writes=['w2'])
                S.dma('sp', w3[:], fw3[l], writes=['w3'])
                S.dma('sp', w4[:], fw4[l], writes=['w4'])
                S.dma('sp', tn[:], tnin, writes=['tn'])
                S.dma('sp', nad[:], nadin, writes=['nad'])
                S.dma('sp', hdr[:], hyd[l], writes=['hdr'])
                S.dma('sp', x0t[:], x0c.rearrange("(ct p) t -> p ct t", p=128), reads=['x0c'], writes=['x0t'])
                for k in range(3):
                    tt('dve', fb[:, k:k + 1], pl[0:64, l, C_FILT + 2 * k:C_FILT + 2 * k + 1], pl[0:64, l, C_FILT + 2 * k + 1:C_FILT + 2 * k + 2],
                       ALU.mult, writes=[('fb', k)])

                def run(Lq, zsrc, tt0, npair, nst, fsrc, gsrc, nr, tok0, uut_rows, is_ctx):
                    nblk = (Lq + 511) // 512
                    S.dma('sp', zf[:, :Lq], zsrc, reads=['zf', 'ha'], writes=['zf'])
                    srcs = [(w1, zf, 33), (w2, ha, 64), (w3, hb_, 64)]
                    dsts = [ha, hb_, ha]
                    for k in range(3):
                        wk, src, kk = srcs[k]
                        dst = dsts[k]
                        fk = pl[0:64, l, C_FILT + 2 * k + 1:C_FILT + 2 * k + 2]
                        for bi in range(nblk):
                            n = min(512, Lq - bi * 512)
                            p = bi % 2
                            S.mm(ps[p][0:64, :n], [(wk[0:kk, :], src[0:kk, bi * 512:bi * 512 + n])],
                                 reads=['w1', 'w2', 'w3', 'zf', ('h', k - 1)], writes=[('fps', p)])
                            ts('dve', tmp[:, p, :n], ps[p][0:64, :n], fk, fb[:, k:k + 1], ALU.mult, ALU.add,
                               reads=[('fps', p), ('fb', k)], writes=[('tmp', p)])
                            for rnd in range(2):
                                for (cmpop, thr, addv) in ((ALU.is_gt, math.pi, -TWO_PI), (ALU.is_lt, -math.pi, TWO_PI)):
                                    ts('dve', tmpm[:, 0, :n], tmp[:, p, :n], thr, addv, cmpop, ALU.mult,
                                       reads=[('tmp', p), 'tmpm'], writes=['tmpm'])
                                    tt('dve', tmp[:, p, :n], tmp[:, p, :n], tmpm[:, 0, :n], ALU.add,
                                       reads=[('tmp', p), 'tmpm'], writes=[('tmp', p)])
                            act(dst[:, bi * 512:bi * 512 + n], tmp[:, p, :n], AF.Sin,
                                reads=[('tmp', p)], writes=[('h', k)])
                    for ti in range(Lq // 128):
                        p2 = 2 + 2 * (ti % 2)
                        q = 0
                        for hh in range(2):
                            S.mm(ps[p2 + hh][:], [(ha[:, ti * 128:(ti + 1) * 128], w4[:, hh * 512:(hh + 1) * 512])],
                                 reads=[('h', 2), 'w4'], writes=[('fps', p2 + hh)])
                        act(dec[:, q, :], nad[:], AF.Exp, scale=tn[:, tt0 + ti:tt0 + ti + 1], reads=['nad', 'tn'], writes=[('dec', q)])
                        for hh in range(2):
                            tt('dve', hfb[:, q, hh, :], ps[p2 + hh][:], dec[:, q, :], ALU.mult,
                               reads=[('fps', p2 + hh), ('dec', q)], writes=[('hfb', q, hh)])
                        if ti == 0:
                            S.op('dve', lambda e: e.memset(hfb[0:1, q, 1, :], 0.0), reads=[('hfb', q, 1)], writes=[('hfb', q, 1)])
                            tt('dve', hfb[0:1, q, 0, :], hfb[0:1, q, 0, :], hdr[:], ALU.add, reads=[('hfb', q, 0), 'hdr'], writes=[('hfb', q, 0)])
                        tt('pool', hs[:, ti, :], hfb[:, q, 0, :], hfb[:, q, 1, :], ALU.add,
                           reads=[('hfb', q, 0), ('hfb', q, 1)], writes=[('hs', ti)])
                        tt('pool', hd[:, ti, :], hfb[:, q, 1, :], hfb[:, q, 0, :], ALU.subtract,
                           reads=[('hfb', q, 0), ('hfb', q, 1)], writes=[('hd', ti)])
                    S.dma('sp', uts[:, 0:nst, :], uut[uut_rows:uut_rows + Lq, :].rearrange("(tt p) c -> p tt c", p=128),
                          reads=['uut', 'uts'], writes=['uts'])
                    hsk = [('hs', t) for t in range(nst)]
                    hdk = [('hd', t) for t in range(nst)]
                    def loadf(i):
                        if i < npair:
                            S.dma('pool', fch[:, i % 2, 0:nst, :], fsrc[i], writes=[('fch', i % 2)])
                    loadf(0)
                    for i in range(npair):
                        slot = i % 2
                        loadf(i + 1)
                        b0 = 4 * (i % 2)
                        S.mm(ps[b0 + 0][:], [(fch[:, slot, s_, 0:128], uts[:, s_, :]) for s_ in range(nst)],
                             reads=[('fch', slot), 'uts'], writes=[('fps', b0)])
                        S.mm(ps[b0 + 1][:], [(fch[:, slot, s_, 128:256], uts[:, s_, :]) for s_ in range(nst)],
                             reads=[('fch', slot), 'uts'], writes=[('fps', b0 + 1)])
                        S.mm(ps[b0 + 2][:], [(fch[:, slot, s_, 0:128], hs[:, s_, :]) for s_ in range(nst)],
                             reads=[('fch', slot)] + hsk, writes=[('fps', b0 + 2)])
                        S.mm(ps[b0 + 3][:], [(fch[:, slot, s_, 128:256], hd[:, s_, :]) for s_ in range(nst)],
                             reads=[('fch', slot)] + hdk, writes=[('fps', b0 + 3)])
                        act(ahs[:, 0, :], ps[b0 + 2][:], AF.Identity, reads=[('fps', b0 + 2)], writes=[('ahs', 0)])
                        act(ahs[:, 1, :], ps[b0 + 3][:], AF.Identity, reads=[('fps', b0 + 3)], writes=[('ahs', 1)])
                        tt('dve', pt[:, 0, :], ps[b0 + 0][:], ahs[:, 0, :], ALU.mult, reads=[('fps', b0), ('ahs', 0)], writes=[('pt', 0)])
                        tt('dve', pt[:, 1, :], ps[b0 + 1][:], ahs[:, 1, :], ALU.mult, reads=[('fps', b0 + 1), ('ahs', 1)], writes=[('pt', 1)])
                        tt('dve', pt[:, 2, :], ps[b0 + 0][:], ahs[:, 1, :], ALU.mult, reads=[('fps', b0), ('ahs', 1)], writes=[('pt', 2)])
                        tt('dve', pt[:, 3, :], ps[b0 + 1][:], ahs[:, 0, :], ALU.mult, reads=[('fps', b0 + 1), ('ahs', 0)], writes=[('pt', 3)])
                        tt('pool', ypk[:, i, :], pt[:, 0, :], pt[:, 1, :], ALU.add, reads=[('pt', 0), ('pt', 1)], writes=[('ypk', i)])
                        tt('pool', ypk[:, npair + i, :], pt[:, 2, :], pt[:, 3, :], ALU.subtract, reads=[('pt', 2), ('pt', 3)],
                           writes=[('ypk', npair + i)])
                        if i == 0:
                            S.mm(ps[b0 + 3][0:1, :], [(fch[:, slot, s_, 128:129], hs[:, s_, :]) for s_ in range(nst)],
                                 reads=[('fch', slot), ('ahs', 1), ('pt', 3)] + hsk, writes=[('fps', b0 + 3)])
                            act(pn[:], ps[b0 + 3][0:1, :], AF.Identity, reads=[('fps', b0 + 3)], writes=['pn'])
                            cp('pool', ypk[0:1, 0, :], pt[0:1, 0, :], reads=[('pt', 0), ('ypk', 0)], writes=[('ypk', 0)])
                            tt('dve', ypk[0:1, npair, :], ps[b0 + 1][0:1, :], pn[:], ALU.mult, reads=[('fps', b0 + 1), 'pn', ('ypk', npair)],
                               writes=[('ypk', npair)])
                    ntok = HALF if not is_ctx else LC
                    nblk_o = (ntok + 511) // 512
                    ypkk = [('ypk', r) for r in range(nr)]
                    if not is_ctx:
                        def loadg(i):
                            if i < nr // 2:
                                S.dma('pool', gch[:, i % 2], gsrc[i * 2:(i + 1) * 2].rearrange("r p t -> p r t"), writes=[('gch', i % 2)])
                        loadg(0)
                        for rc in range(nr // 2):
                            slot = rc % 2
                            loadg(rc + 1)
                            instrs = []
                            for rr in range(2):
                                r = rc * 2 + rr
                                for ct in range(4):
                                    for tb in range(2):
                                        instrs.append((ps[ct * 2 + tb][:], ypk[:, r, ct * 128:(ct + 1) * 128], gch[:, slot, rr, tb * 512:(tb + 1) * 512],
                                                       r == 0, r == nr - 1))
                            S.mmv(instrs, reads=[('gch', slot)] + ypkk, writes=[('fps', k) for k in range(8)])
                        for ct in range(4):
                            for tb in range(2):
                                tt('dve', ybo[:, ct, tb * 512:(tb + 1) * 512], ps[ct * 2 + tb][:], x0t[:, ct, tb * 512:(tb + 1) * 512], ALU.mult,
                                   reads=[('fps', ct * 2 + tb), 'x0t'], writes=[('ybo', ct)])
                    else:
                        for sl in range(2):
                            S.dma('pool', gch[:, sl, :, 0:LC], gsrc[sl * 2:(sl + 1) * 2].rearrange("r p t -> p r t"), writes=[('gch', sl)])
                        instrs = []
                        for r in range(nr):
                            for ct in range(4):
                                instrs.append((ps[ct][:, 0:LC], ypk[:, r, ct * 128:(ct + 1) * 128], gch[:, r // 2, r % 2, 0:LC], r == 0, r == nr - 1))
                        S.mmv(instrs, reads=[('gch', 0), ('gch', 1)] + ypkk, writes=[('fps', k) for k in range(4)])
                        for ct in range(4):
                            tt('dve', ybo[:, ct, HALF:NT], ps[ct][:, 0:LC], x0t[:, ct, HALF:NT], ALU.mult,
                               reads=[('fps', ct), 'x0t'], writes=[('ybo', ct)])

                run(L, zft, 0, 16, 16, fm, gm, 32, 0, 0, False)
                run(LC, zftc, 16, 2, 2, fmc, gmc, 4, HALF, L, True)
                S.dma('sp', ycat[512:1024, :].rearrange("(ct p) t -> p ct t", p=128), ybo[:], reads=[('ybo', c) for c in range(4)],
                      writes=['ycat_b'])
            S.barrier()

        def stage_ret(l):
            with ExitStack() as st:
                sb = lambda name, shape, dt: st.enter_context(nc.sbuf_tensor(uq(name), list(shape), dt))
                kh = sb("rkh", [128, 2, L + LC], BF16)
                vh = sb("rvh", [128, 18, 256], BF16)
                qh = sb("rqh", [128, 2, NT], BF16)
                gh = sb("rgh", [128, 2, NT], BF16)
                rel = sb("rrel", [128, RELW], F32)
                relc = sb("rrelc", [128, 384], F32)
                msk = sb("rmsk", [128, RELW], F32)
                mskc = sb("rmskc", [128, 384], F32)
                ta = sb("rta", [128, RELW], F32)
                relp = sb("rrelp", [128, RELW], F32)
                reln = sb("rreln", [128, RELW], F32)
                relcp = sb("rrelcp", [128, 384], F32)
                relcn = sb("rrelcn", [128, 384], F32)
                lg = sb("rlg", [128, 8], F32)
                ptT = sb("rpt", [128, 2, 512], BF16)
                stt_ = sb("rstat", [128, 2, 6], F32)
                mv = sb("rmv", [128, 2, 2], F32)
                rstd = sb("rrstd", [128, 2, 1], F32)
                on = sb("ron", [128, 2, 256], BF16)
                yco = sb("ryco", [128, 2, NT], BF16)
                oc = sb("roc", [128, 2, 4, 256], F32)
                pend = []
                bset = [0]
                pss = [st.enter_context(nc.psum_tensor(uq(f"rps{i}"), [128, 512], F32)) for i in range(2)]
                pso = [st.enter_context(nc.psum_tensor(uq(f"rpo{i}"), [128, 512], F32)) for i in range(4)]
                pstr = [st.enter_context(nc.psum_tensor(uq(f"rpt{i}"), [128, 2, 128], BF16)) for i in range(2)]
                def defer_finish(q0, nq):
                    s_ = bset[0]
                    bset[0] ^= 1
                    for qi in range(nq):
                        act(oc[:, s_, qi, :], pso[qi][:, 0:256], AF.Identity, reads=[('pso', qi)], writes=[('oc', s_, qi)])
                    while pend:
                        pend.pop(0)()
                    pend.append(lambda: finish_queries(q0, nq, oc, s_, stt_, mv, rstd, on, pstr, gh, yco))

                S.dma('sp', rel[:], relT, writes=['rel'])
                S.dma('sp', relc[:], relC, writes=['relc'])
                act(lg[:], pl[:, l, C_RET:C_RET + 8], AF.Exp, writes=['lg'])
                act(lg[:], lg[:], AF.Ln, bias=cst[:, 2:3], scale=-1.0, reads=['lg'], writes=['lg1'])
                ts('dve', relp[:], rel[:], 0.0, None, ALU.max, reads=['rel'], writes=['relp'])
                ts('pool', reln[:], rel[:], -1.0, 0.0, ALU.mult, ALU.max, reads=['rel'], writes=['reln'])
                ts('dve', relcp[:], relc[:], 0.0, None, ALU.max, reads=['relc'], writes=['relp'])
                ts('pool', relcn[:], relc[:], -1.0, 0.0, ALU.mult, ALU.max, reads=['relc'], writes=['reln'])
                for hh in range(4):
                    for (rsrc, rp, rn, mdst, Wd, key) in ((rel, relp, reln, msk, RELW, 'm'), (relc, relcp, relcn, mskc, 384, 'mc')):
                        act(ta[:, :Wd], rp[:, :Wd], AF.Identity, scale=lg[:, hh:hh + 1], reads=['relp', 'lg1', 'ta'], writes=['ta'])
                        stt('dve', ta[:, :Wd], rn[:, :Wd], lg[:, 4 + hh:5 + hh], ta[:, :Wd], ALU.mult, ALU.add,
                            reads=['reln', 'lg1', 'ta'], writes=['ta'])
                        act(ta[:, :Wd], ta[:, :Wd], AF.Exp, reads=['ta'], writes=['ta'])
                        stt('dve', mdst[:, :Wd], rsrc[:, :Wd], 0.0, ta[:, :Wd], ALU.is_equal, ALU.add,
                            reads=['ta', 'rel', 'relc', key], writes=[key])
                    S.dma('sp', kh[:], zk[hh * 256:(hh + 1) * 256, :].rearrange("(dt p) t -> p dt t", p=128), reads=['zk', 'kh'], writes=['kh'])
                    S.dma('sp', vh[:], zv[:, hh * 256:(hh + 1) * 256].rearrange("(tc p) c -> p tc c", p=128), reads=['zv', 'vh'], writes=['vh'])
                    S.dma('sp', qh[:], zq[hh * 256:(hh + 1) * 256, :].rearrange("(dt p) t -> p dt t", p=128), reads=['zq', 'qh'], writes=['qh'])
                    S.dma('sp', gh[:], zg[hh * 256:(hh + 1) * 256, :].rearrange("(dt p) t -> p dt t", p=128), reads=['zq', 'gh'], writes=['gh'])
                    ci = 0
                    for qb in range(2):
                        q0, n, nq = qb * 512, 512, 4
                        kl = [(L + 128 * m, 16 + m, -256 + 128 * m) for m in range(2)]
                        kl += [(128 * j, j, 128 * j) for j in range(16)]
                        kl += [(L + 128 * m, 16 + m, 2048 + 128 * m) for m in range(2)]
                        for idx, (kcol, vchunk, kbase) in enumerate(kl):
                            p = ci % 2
                            ci += 1
                            S.mm(pss[p][:, :n], [(kh[:, dt_, kcol:kcol + 128], qh[:, dt_, q0:q0 + n]) for dt_ in range(2)],
                                 reads=['kh', 'qh'], writes=[('pss', p)])
                            moff = q0 - kbase + 2304
                            assert 0 <= moff and moff + n <= RELW
                            tt('dve', ptT[:, p, :n], pss[p][:, :n], msk[:, moff:moff + n], ALU.mult,
                               reads=[('pss', p), 'm'], writes=[('ptT', p)])
                            instrs = []
                            for qi in range(nq):
                                instrs.append((pso[qi][:, 0:256], ptT[:, p, qi * 128:(qi + 1) * 128], vh[:, vchunk, :],
                                               idx == 0, idx == len(kl) - 1))
                            S.mmv(instrs, reads=[('ptT', p), 'vh'], writes=[('pso', q_) for q_ in range(4)])
                        defer_finish(q0, nq)
                    for m in range(2):
                        p = ci % 2
                        ci += 1
                        S.mm(pss[p][:, :LC], [(kh[:, dt_, L + 128 * m:L + 128 * (m + 1)], qh[:, dt_, HALF:NT]) for dt_ in range(2)],
                             reads=['kh', 'qh'], writes=[('pss', p)])
                        moff = 128 - 128 * m
                        tt('dve', ptT[:, p, :LC], pss[p][:, :LC], mskc[:, moff:moff + LC], ALU.mult,
                           reads=[('pss', p), 'mc'], writes=[('ptT', p)])
                        instrs = []
                        for qi in range(2):
                            instrs.append((pso[qi][:, 0:256], ptT[:, p, qi * 128:(qi + 1) * 128], vh[:, 16 + m, :], m == 0, m == 1))
                        S.mmv(instrs, reads=[('ptT', p), 'vh'], writes=[('pso', q_) for q_ in range(4)])
                    defer_finish(HALF, 2)
                    while pend:
                        pend.pop(0)()
                    S.dma('sp', ycat[1024 + hh * 256:1024 + (hh + 1) * 256, :].rearrange("(dt p) t -> p dt t", p=128), yco[:],
                          reads=[('yco', 0), ('yco', 1)], writes=['ycat_c'])
            S.barrier()

        def finish_queries(q0, nq, oc, s_, stt_, mv, rstd, on, pstr, gh, yco):
            for qi in range(nq):
                o = oc[:, s_, qi, :]
                k = qi % 2
                S.op('dve', lambda e: e.bn_stats(out=stt_[:, k, :], in_=o), reads=[('oc', s_, qi)], writes=[('stat', k)])
                S.op('dve', lambda e: e.bn_aggr(out=mv[:, k, :], in_=stt_[:, k, :]), reads=[('stat', k)], writes=[('mv', k)])
                act(rstd[:, k, :], mv[:, k, 1:2], AF.Sqrt, bias=cst[:, 4:5], reads=[('mv', k)], writes=[('rstd', k)])
                S.op('dve', lambda e: e.reciprocal(out=rstd[:, k, :], in_=rstd[:, k, :]), reads=[('rstd', k)], writes=[('rstd', k)])
                ts('dve', on[:, k, :], o, mv[:, k, 0:1], rstd[:, k, :], ALU.subtract, ALU.mult,
                   reads=[('oc', s_, qi), ('mv', k), ('rstd', k)], writes=[('on', k)])
                for dt_ in range(2):
                    S.op('pe', lambda e, dt_=dt_: e.transpose(pstr[k][:, dt_, :], on[:, k, dt_ * 128:(dt_ + 1) * 128], identb[:]),
                         reads=[('on', k)], writes=[('pstr', k)])
                    tt('dve', yco[:, dt_, q0 + qi * 128:q0 + (qi + 1) * 128], pstr[k][:, dt_, :], gh[:, dt_, q0 + qi * 128:q0 + (qi + 1) * 128],
                       ALU.mult, reads=[('pstr', k), 'gh'], writes=[('yco', dt_)])

        def stage_merge(l):
            TBL = TB3[:2] if l == DEPTH - 1 else TB3
            with ExitStack() as st:
                sb = lambda name, shape, dt: st.enter_context(nc.sbuf_tensor(uq(name), list(shape), dt))
                hoc = sb("mhoc", [128, 16, NT], BF16)
                yct = sb("myct", [128, 16, NT], BF16)
                wg = sb("mwg", [128, 2, 3, 16, 256], BF16)
                wp = sb("mwp", [128, 2, 16, 256], BF16)
                gs = sb("mgs", [128, 3, 512], F32)
                m1 = sb("mm1", [128, 3, 512], F32)
                mgo = sb("mmgo", [128, 2, 2, 512], BF16)
                psg = [st.enter_context(nc.psum_tensor(uq(f"mpg{i}"), [128, 512], F32)) for i in range(3)]
                psp = [st.enter_context(nc.psum_tensor(uq(f"mpp{i}"), [128, 512], F32)) for i in range(3)]
                for c in range(2):
                    S.dma('sp', hoc[:, c * 8:(c + 1) * 8, 0:HALF], hx[c].rearrange("(kt p) t -> p kt t", p=128), reads=['hx'], writes=['hoc'])
                S.dma('sp', hoc[:, :, HALF:NT], hctx.rearrange("(kt p) t -> p kt t", p=128), reads=['hctx'], writes=['hoc'])
                S.dma('sp', yct[:], ycat.rearrange("(kt p) t -> p kt t", p=128), reads=['ycat_a', 'ycat_b', 'ycat_c'], writes=['yct'])
                si = 0
                kr = [(0, 4), (4, 8), (8, 16)]
                def load(i):
                    if i >= 8:
                        return
                    sl, nn0 = i % 2, i * 256
                    for br in range(3):
                        cc0 = O_GATE + br * D + nn0
                        S.dma('pool', wg[:, sl, br], w_in[l, :, cc0:cc0 + 256].rearrange("(kt p) c -> p kt c", p=128),
                              writes=[('wg', sl, br)])
                    S.dma('pool', wp[:, sl, 0:4], p_a[l, :, nn0:nn0 + 256].rearrange("(kt p) c -> p kt c", p=128), writes=[('wp', sl, 0)])
                    S.dma('pool', wp[:, sl, 4:8], p_b[l, :, nn0:nn0 + 256].rearrange("(kt p) c -> p kt c", p=128), writes=[('wp', sl, 1)])
                    S.dma('pool', wp[:, sl, 8:16], p_c[l, :, nn0:nn0 + 256].rearrange("(kt p) c -> p kt c", p=128), writes=[('wp', sl, 2)])
                load(0)
                for nci in range(8):
                    slot = nci % 2
                    n0 = nci * 256
                    load(nci + 1)
                    for (t0, n, row) in TBL:
                        ss = si % 2
                        si += 1
                        for nt in range(2):
                            N = nci * 2 + nt
                            for br in range(3):
                                gcol = (O_GATE // 128) + br * 16 + N
                                S.mm(psg[br][:, :n], [(wg[:, slot, br, kt, nt * 128:(nt + 1) * 128], hoc[:, kt, t0:t0 + n]) for kt in range(16)],
                                     reads=[('wg', slot, br), 'hoc'], writes=[('psg', br)])
                                act(gs[:, br, :n], psg[br][:, :n], AF.Sigmoid, bias=pl[:, l, C_BIN + gcol:C_BIN + gcol + 1],
                                    reads=[('psg', br)], writes=[('gs', br)])
                                k0, k1 = kr[br]
                                S.mm(psp[br][:, :n], [(wp[:, slot, kt, nt * 128:(nt + 1) * 128], yct[:, kt, t0:t0 + n]) for kt in range(k0, k1)],
                                     reads=[('wp', slot, br), 'yct'], writes=[('psp', br)])
                                tt('dve', m1[:, br, :n], psp[br][:, :n], gs[:, br, :n], ALU.mult,
                                   reads=[('psp', br), ('gs', br)], writes=[('m1', br)])
                            tt('pool', m1[:, 0, :n], m1[:, 0, :n], m1[:, 1, :n], ALU.add, reads=[('m1', 0), ('m1', 1)], writes=[('m1', 0)])
                            tt('pool', mgo[:, ss, nt, :n], m1[:, 0, :n], m1[:, 2, :n], ALU.add, reads=[('m1', 0), ('m1', 2)], writes=[('mgo', ss, nt)])
                        S.dma('sp', mg[n0:n0 + 256, t0:t0 + n].rearrange("(nt p) t -> p nt t", p=128), mgo[:, ss, :, :n],
                              reads=[('mgo', ss, 0), ('mgo', ss, 1)], writes=['mg'])
            S.barrier()

        def layer_norm(r, sq, stt4, psA, psB, n, l, gcol, bcol):
            mean, msq, var, rstd = (stt4[:, i, :n] for i in range(4))
            for N in range(16):
                act(sq[:, N, :n], r[:, N, :n], AF.Square, reads=[('r', N)], writes=[('sq', N)])
            S.mm(psA[:, :n], [(onesd[:], r[:, N, :n]) for N in range(16)], reads=[('r', N) for N in range(16)], writes=['psA'])
            S.mm(psB[:, :n], [(onesd[:], sq[:, N, :n]) for N in range(16)], reads=[('sq', N) for N in range(16)], writes=['psB'])
            cp('dve', mean, psA[:, :n], reads=['psA'], writes=['mean'])
            tt('dve', msq, mean, mean, ALU.mult, reads=['mean'], writes=['msq'])
            tt('dve', var, psB[:, :n], msq, ALU.subtract, reads=['psB', 'msq'], writes=['var'])
            act(rstd, var, AF.Sqrt, bias=cst[:, 3:4], reads=['var'], writes=['rstd'])
            S.op('dve', lambda e: e.reciprocal(out=rstd, in_=rstd), reads=['rstd'], writes=['rstd'])
            for N in range(16):
                tt('dve', r[:, N, :n], r[:, N, :n], mean, ALU.subtract, reads=[('r', N), 'mean'], writes=[('r', N)])
                tt('pool', r[:, N, :n], r[:, N, :n], rstd, ALU.mult, reads=[('r', N), 'rstd'], writes=[('r', N)])
                act(r[:, N, :n], r[:, N, :n], AF.Identity, bias=pl[:, l, bcol + N:bcol + N + 1], scale=pl[:, l, gcol + N:gcol + N + 1],
                    reads=[('r', N)], writes=[('r', N)])

        def stage_wo_ln(l):
            TBL = TB3[:2] if l == DEPTH - 1 else TB3
            with ExitStack() as st:
                sb = lambda name, shape, dt: st.enter_context(nc.sbuf_tensor(uq(name), list(shape), dt))
                mgt = sb("omgt", [128, 16, NT], BF16)
                wo = sb("owo", [128, 2, 16, 512], BF16)
                r = sb("or", [128, 16, 512], F32)
                sq = sb("osq", [128, 16, 512], F32)
                xt = sb("oxt", [128, 16, 512], F32)
                stt4 = sb("ost4", [128, 4, 512], F32)
                hbo = sb("ohbo", [128, 16, 512], BF16)
                ps = [st.enter_context(nc.psum_tensor(uq(f"ops{i}"), [128, 512], F32)) for i in range(4)]
                psA = st.enter_context(nc.psum_tensor(uq("opsA"), [128, 512], F32))
                psB = st.enter_context(nc.psum_tensor(uq("opsB"), [128, 512], F32))
                S.dma('sp', mgt[:], mg.rearrange("(kt p) t -> p kt t", p=128), reads=['mg'], writes=['mgt'])
                ci = 0
                pi = 0

                def load(i):
                    if i < 4 * len(TBL):
                        c_ = i % 4
                        S.dma('pool', wo[:, i % 2], w_o[l, :, c_ * 512:(c_ + 1) * 512].rearrange("(kt p) c -> p kt c", p=128),
                              writes=[('wo', i % 2)])
                load(0)
                for (t0, n, row) in TBL:
                    S.dma('sp', xt[:, :, :n], xres[:, t0:t0 + n].rearrange("(nt p) t -> p nt t", p=128),
                          reads=['xres', 'xt'] + [('r', N) for N in range(16)], writes=['xt'])
                    for c in range(4):
                        slot = ci % 2
                        ci += 1
                        load(ci)
                        for ct in range(4):
                            N = c * 4 + ct
                            p = pi % 4
                            pi += 1
                            S.mm(ps[p][:, :n], [(wo[:, slot, kt, ct * 128:(ct + 1) * 128], mgt[:, kt, t0:t0 + n]) for kt in range(16)],
                                 reads=[('wo', slot), 'mgt'], writes=[('ops', p)])
                            act(r[:, N, :n], ps[p][:, :n], AF.Identity, bias=der[:, row * 16 + N:row * 16 + N + 1],
                                scale=mod[:, l, 32 + N, row:row + 1], reads=[('ops', p), 'hbo_dma', 'xres_dma'], writes=[('r', N)])
                            stt('dve', r[:, N, :n], xt[:, N, :n], ALPHA, r[:, N, :n], ALU.mult, ALU.add,
                                reads=['xt', ('r', N)], writes=[('r', N)])
                    layer_norm(r, sq, stt4, psA, psB, n, l, C_L1G, C_L1B)
                    for N in range(16):
                        if N % 2 == 0:
                            act(hbo[:, N, :n], r[:, N, :n], AF.Identity, bias=mod[:, l, 48 + N, row:row + 1],
                                scale=mod1[:, l, 64 + N, row:row + 1], reads=[('r', N), 'hbo_dma'], writes=[('hbo', N)])
                        else:
                            ts('dve', hbo[:, N, :n], r[:, N, :n], mod1[:, l, 64 + N, row:row + 1], mod[:, l, 48 + N, row:row + 1],
                               ALU.mult, ALU.add, reads=[('r', N), 'hbo_dma'], writes=[('hbo', N)])
                    S.dma('sp', xres[:, t0:t0 + n].rearrange("(nt p) t -> p nt t", p=128), r[:, :, :n],
                          reads=[('r', N) for N in range(16)] + ['xt'], writes=['xres_dma'])
                    S.dma('sp', h2[:, t0:t0 + n].rearrange("(nt p) t -> p nt t", p=128), hbo[:, :, :n],
                          reads=[('hbo', N) for N in range(16)], writes=['hbo_dma'])
            S.barrier()

        def stage_mlp(l):
            TBL = TB3[:2] if l == DEPTH - 1 else TB3
            with ExitStack() as st:
                sb = lambda name, shape, dt: st.enter_context(nc.sbuf_tensor(uq(name), list(shape), dt))
                acc = sb("pacc", [128, 16, NT], F32)
                h2t = sb("ph2t", [128, 16, NT], BF16)
                w1c = sb("pw1c", [128, 2, 16, 256], BF16)
                w2c = sb("pw2c", [128, 2, 2, D], BF16)
                hid = sb("phid", [128, 2, 2, NT], BF16)
                tmp1 = sb("ptmp1", [128, 2, 512], F32)
                tmp2 = sb("ptmp2", [128, 2, 512], F32)
                ps1 = [st.enter_context(nc.psum_tensor(uq(f"pp1{i}"), [128, 512], F32)) for i in range(2)]
                ps2 = [st.enter_context(nc.psum_tensor(uq(f"pp2{i}"), [128, 512], F32)) for i in range(6)]
                S.dma('sp', h2t[:], h2.rearrange("(kt p) t -> p kt t", p=128), reads=['h2'], writes=['h2t'])
                S.dma('sp', acc[:], xres.rearrange("(nt p) t -> p nt t", p=128), reads=['xres'], writes=['acc0'])
                for N in range(16):
                    for (t0, n, row) in TBL:
                        act(acc[:, N, t0:t0 + n], acc[:, N, t0:t0 + n], AF.Identity, bias=der[:, 32 + row * 16 + N:32 + row * 16 + N + 1],
                            scale=ALPHA, reads=['acc0'], writes=[('acc', N, t0)])
                i1 = 0
                i2 = 0
                i3 = 0
                def load(i):
                    if i < DFF // 256:
                        S.dma('pool', w1c[:, i % 2], w_m1[l, :, i * 256:(i + 1) * 256].rearrange("(kt p) c -> p kt c", p=128),
                              writes=[('w1c', i % 2)])
                        S.dma('pool', w2c[:, i % 2], w_m2[l, i * 256:(i + 1) * 256, :].rearrange("(kt p) c -> p kt c", p=128),
                              writes=[('w2c', i % 2)])
                load(0)
                for j in range(DFF // 256):
                    slot = j % 2
                    load(j + 1)
                    for ft in range(2):
                        bcol = C_B1 + j * 2 + ft
                        for (t0, n, row) in TBL:
                            p = i1 % 2
                            i1 += 1
                            S.mm(ps1[p][:, :n], [(w1c[:, slot, kt, ft * 128:(ft + 1) * 128], h2t[:, kt, t0:t0 + n]) for kt in range(16)],
                                 reads=[('w1c', slot), 'h2t'], writes=[('ps1', p)])
                            ts('dve', tmp1[:, p, :n], ps1[p][:, :n], pl[:, l, bcol:bcol + 1], 0.0, ALU.add, ALU.max,
                               reads=[('ps1', p)], writes=[('tmp1', p)])
                            act(hid[:, slot, ft, t0:t0 + n], tmp1[:, p, :n], AF.Square, reads=[('tmp1', p)], writes=[('hid', slot, ft, t0)])
                    for N in range(16):
                        for (t0, n, row) in TBL:
                            p = i2 % 6
                            i2 += 1
                            S.mm(ps2[p][:, :n], [(w2c[:, slot, kt, N * 128:(N + 1) * 128], hid[:, slot, kt, t0:t0 + n]) for kt in range(2)],
                                 reads=[('w2c', slot), ('hid', slot, 0, t0), ('hid', slot, 1, t0)], writes=[('ps2', p)])
                            g2 = mod[:, l, 80 + N, row:row + 1]
                            if i2 % 8 not in (1, 4, 6):
                                stt('dve', acc[:, N, t0:t0 + n], ps2[p][:, :n], g2, acc[:, N, t0:t0 + n], ALU.mult, ALU.add,
                                    reads=[('ps2', p), ('acc', N, t0)], writes=[('acc', N, t0)])
                            else:
                                q = i3 % 2
                                i3 += 1
                                act(tmp2[:, q, :n], ps2[p][:, :n], AF.Identity, scale=g2, reads=[('ps2', p)], writes=[('tmp2', q)])
                                tt('pool', acc[:, N, t0:t0 + n], acc[:, N, t0:t0 + n], tmp2[:, q, :n], ALU.add,
                                   reads=[('tmp2', q), ('acc', N, t0)], writes=[('acc', N, t0)])
                S.dma('sp', xres.rearrange("(nt p) t -> p nt t", p=128), acc[:],
                      reads=[('acc', N, t0) for N in range(16) for (t0, _, _) in TB3], writes=['xres'])
            S.barrier()

        def stage_ln2(l, last):
            with ExitStack() as st:
                sb = lambda name, shape, dt: st.enter_context(nc.sbuf_tensor(uq(name), list(shape), dt))
                r = sb("lr", [128, 2, 16, 512], F32)
                sq = sb("lsq", [128, 16, 512], F32)
                stt4 = sb("lst4", [128, 4, 512], F32)
                psA = st.enter_context(nc.psum_tensor(uq("lpsA"), [128, 512], F32))
                psB = st.enter_context(nc.psum_tensor(uq("lpsB"), [128, 512], F32))
                for i, (t0, n, row) in enumerate(TB3):
                    if last and row == 1:
                        continue
                    rr = r[:, i % 2]
                    S.dma('sp', rr[:, :, :n], xres[:, t0:t0 + n].rearrange("(nt p) t -> p nt t", p=128),
                          reads=['xres'] + [('r', N) for N in range(16)], writes=[('r', N) for N in range(16)])
                    layer_norm(rr, sq, stt4, psA, psB, n, l, C_L2G, C_L2B)
                    dst = yout[:, t0:t0 + n] if last else xres[:, t0:t0 + n]
                    S.dma('sp', dst.rearrange("(nt p) t -> p nt t", p=128), rr[:, :, :n],
                          reads=[('r', N) for N in range(16)], writes=['xres_o'])
            S.barrier()

        stage_ada()
        if stop_after == 'ada':
            depth = 0
        stages = [("derive", stage_derive), ("h", stage_h), ("inproj_a", stage_inproj_a), ("inproj_b", stage_inproj_b),
                  ("pool", stage_pool), ("hyfront", stage_hyfront), ("hyena", stage_hyena), ("ret", stage_ret),
                  ("merge", stage_merge), ("wo_ln", stage_wo_ln), ("mlp", stage_mlp)]
        done = False
        for l in range(depth):
            for nm, fn in stages:
                fn(l)
                if stop_after == nm:
                    done = True
                    break
            if done:
                break
            stage_ln2(l, last=(l == DEPTH - 1))
        S.barrier()
    return nc


def _fm(t):
    return np.ascontiguousarray(np.asarray(t, np.float32).reshape(-1, 128).T)


_CONST_CACHE = {}


def _tables():
    if _CONST_CACHE:
        return _CONST_CACHE
    C = _CONST_CACHE
    f64 = np.float64
    for name, Lq in (("zft", L), ("zftc", LC)):
        t = np.linspace(0.0, 1.0, Lq, dtype=np.float32)[:, None]
        w = (2.0 * math.pi * np.arange(Lq, dtype=np.float32)[:, None] / Lq).astype(np.float32)
        f = np.linspace(1e-4, 15, 16, dtype=np.float32)[None, :]
        z = np.concatenate([t, np.cos(f * w), -np.sin(f * w)], axis=-1).astype(np.float32)
        C[name] = np.ascontiguousarray(z.T)
    tn = np.zeros((128, 18), np.float32)
    for tt_ in range(16):
        tn[:, tt_] = np.linspace(0.0, 1.0, L, dtype=np.float32)[tt_ * 128:(tt_ + 1) * 128]
    for j in range(2):
        tn[:, 16 + j] = np.linspace(0.0, 1.0, LC, dtype=np.float32)[j * 128:(j + 1) * 128]
    C["tn"] = tn
    max_decay = math.log(1e-2) / 0.3
    min_decay = math.log(1e-2) / 1.5
    deltas = np.linspace(min_decay, max_decay, 512, dtype=np.float32)
    C["nad"] = np.ascontiguousarray(np.broadcast_to(-np.abs(deltas)[None, :], (128, 512))).astype(np.float32)

    def dft_tables(Lq):
        N = 2 * Lq
        nf = Lq
        s = np.arange(Lq)
        f = np.arange(nf)
        ang = 2.0 * np.pi * ((np.outer(s, f)) % N).astype(f64) / N
        Cm = np.cos(ang)
        Sm = np.sin(ang)
        Sm[:, 0] = np.where(s % 2 == 0, 1.0, -1.0)
        return Cm, Sm, N

    Cm, Sm, N = dft_tables(L)
    fm = np.zeros((16, 128, 16, 256), np.float32)
    for i in range(16):
        blkc = Cm[:, i * 128:(i + 1) * 128].reshape(16, 128, 128)
        blks = Sm[:, i * 128:(i + 1) * 128].reshape(16, 128, 128)
        fm[i, :, :, 0:128] = blkc.transpose(1, 0, 2)
        fm[i, :, :, 128:256] = blks.transpose(1, 0, 2)
    C["fm"] = fm
    Cc, Sc, Nc = dft_tables(LC)
    fmc = np.zeros((2, 128, 2, 256), np.float32)
    for i in range(2):
        fmc[i, :, :, 0:128] = Cc[:, i * 128:(i + 1) * 128].reshape(2, 128, 128).transpose(1, 0, 2)
        fmc[i, :, :, 128:256] = Sc[:, i * 128:(i + 1) * 128].reshape(2, 128, 128).transpose(1, 0, 2)
    C["fmc"] = fmc

    def inv_table(Lq, tpos):
        N = 2 * Lq
        f = np.arange(Lq)
        ang = 2.0 * np.pi * ((np.outer(f, tpos)) % N).astype(f64) / N
        a = np.full((Lq, 1), 2.0 / N)
        a[0, 0] = 1.0 / N
        Gc = a * np.cos(ang)
        Gs = -(2.0 / N) * np.sin(ang)
        Gs[0, :] = (1.0 / N) * np.where(tpos % 2 == 0, 1.0, -1.0)
        return np.concatenate([Gc, Gs], axis=0).astype(np.float32)

    C["gm"] = [inv_table(L, np.arange(s * HALF, (s + 1) * HALF)).reshape(32, 128, HALF) for s in range(2)]
    C["gmc"] = inv_table(LC, np.arange(LC)).reshape(4, 128, LC)
    inv = (10000.0 ** (-np.arange(64, dtype=np.float32) / 64)).astype(np.float32)
    tpos = np.arange(L)
    row = (tpos // 64).astype(np.float32)
    col = (tpos % 64).astype(np.float32)
    ang_r = (row[None, :] * inv[:, None]).astype(np.float32)
    ang_c = (col[None, :] * inv[:, None]).astype(np.float32)
    rp = np.zeros((128, 4, L), np.float32)
    rp[:, 0] = np.tile(np.cos(ang_r), (2, 1))
    rp[:, 1] = np.tile(np.cos(ang_c), (2, 1))
    rp[:, 2] = np.tile(np.sin(ang_r), (2, 1))
    rp[:, 3] = np.tile(np.sin(ang_c), (2, 1))
    C["ropeg"] = rp
    C["ropeo"] = [np.ascontiguousarray(rp[:, :, s * HALF:(s + 1) * HALF]) for s in range(2)]
    pm = np.zeros((128, 128), np.float32)
    for m in range(64):
        pm[m + 64, m] = -1.0
        pm[m, m + 64] = 1.0
    C["pm"] = pm
    C["ident"] = np.eye(128, dtype=np.float32)
    wins = (2, 4, 8, 16)
    pinv = []
    for s in range(2):
        a = np.zeros((4, NT), np.float32)
        for g, w in enumerate(wins):
            t = np.arange(s * HALF, (s + 1) * HALF)
            lo = np.clip(t - w // 2, 0, L)
            hi = np.clip(t + w - w // 2, 0, L)
            a[g, :HALF] = 1.0 / (hi - lo)
            t = np.arange(LC)
            lo = np.clip(t - w // 2, 0, LC)
            hi = np.clip(t + w - w // 2, 0, LC)
            a[g, HALF:] = 1.0 / (hi - lo)
        pinv.append(np.ascontiguousarray(np.broadcast_to(a[None], (128, 4, NT))).astype(np.float32))
    C["pinv"] = pinv
    C["halom"] = [np.ascontiguousarray(np.broadcast_to(np.array([[0.0, 1.0]], np.float32), (128, 2))),
                  np.ascontiguousarray(np.broadcast_to(np.array([[1.0, 0.0]], np.float32), (128, 2)))]
    p = np.arange(128, dtype=np.float32)[:, None]
    u = np.arange(RELW, dtype=np.float32)[None, :]
    C["relT"] = [(u - 2304.0 + s * HALF - p).astype(np.float32) for s in range(2)]
    C["relC"] = (np.arange(384, dtype=np.float32)[None, :] - 128.0 - p).astype(np.float32)
    return C


def _layer_params(inp):
    pl = np.zeros((128, DEPTH, NPL), np.float32)
    for l in range(DEPTH):
        pl[:, l, C_BADA:C_BADA + 96] = _fm(inp["b_ada"][l])
        pl[:, l, C_BIN:C_BIN + 96] = _fm(inp["b_in"][l])
        pl[:, l, C_BO:C_BO + 16] = _fm(inp["b_o"][l])
        pl[:, l, C_L1G:C_L1G + 16] = _fm(inp["ln1_g"][l])
        pl[:, l, C_L1B:C_L1B + 16] = _fm(inp["ln1_b"][l])
        pl[:, l, C_B2:C_B2 + 16] = _fm(inp["b_mlp2"][l])
        pl[:, l, C_L2G:C_L2G + 16] = _fm(inp["ln2_g"][l])
        pl[:, l, C_L2B:C_L2B + 16] = _fm(inp["ln2_b"][l])
        pl[:, l, C_B1:C_B1 + 64] = _fm(inp["b_mlp1"][l])
        pl[:, l, C_CB:C_CB + 12] = _fm(inp["conv_b"][l])
        for k in range(3):
            pl[:, l, C_CW + 12 * k:C_CW + 12 * (k + 1)] = _fm(inp["conv_w"][l, k])
        pl[:, l, C_PS:C_PS + 4] = _fm(inp["pool_scale"][l])
        for k, (bn, fn) in enumerate((("filt_b1", "filt_f1"), ("filt_b2", "filt_f2"), ("filt_b3", "filt_f3"))):
            pl[0:64, l, C_FILT + 2 * k] = inp[bn][l]
            pl[0:64, l, C_FILT + 2 * k + 1] = inp[fn][l]
        pl[:, l, C_RET:C_RET + 8] = np.asarray(inp["ret_decay"][l], np.float32).reshape(1, 8)
    return pl


def make_in_maps(inp, depth=DEPTH):
    C = _tables()
    inp = {k: np.asarray(v) for k, v in inp.items()}
    pl = _layer_params(inp)
    bvbc = np.ascontiguousarray(np.broadcast_to(inp["b_in"][:, None, O_V:O_G], (DEPTH, 128, 1024))).astype(np.float32)
    hyd = np.ascontiguousarray(inp["hyena_d"].reshape(DEPTH, 1, 512)).astype(np.float32)
    shared = {k: np.ascontiguousarray(inp[k][:depth], dtype=np.float32) for k in
              ("w_ada", "w_in", "pool_w", "filt_w1", "filt_w2", "filt_w3", "filt_w4", "p_a", "p_b", "p_c", "w_o", "w_mlp1", "w_mlp2")}
    shared.update(pl=pl, bvbc=bvbc[:depth], hyd=hyd[:depth], zft=C["zft"], zftc=C["zftc"], tn=C["tn"], nad=C["nad"], fm=C["fm"], fmc=C["fmc"],
                  gmc=C["gmc"], ropeg=C["ropeg"], pm=C["pm"], ident=C["ident"], relC=C["relC"])
    maps = []
    for core in range(8):
        b, s = core // 2, core % 2
        m = dict(shared)
        xin = np.empty((D, NT), np.float32)
        xin[:, :HALF] = inp["x"][b, s * HALF:(s + 1) * HALF, :].T
        xin[:, HALF:] = inp["ctx"][b].T
        m["xin"] = xin
        cv = np.empty((128, 16, 2), np.float32)
        cv[:, :, 0] = _fm(inp["c"][b])
        cv[:, :, 1] = _fm(inp["c_ctx"])
        m["cvec"] = cv
        m["gm"] = C["gm"][s]
        m["ropeo"] = C["ropeo"][s]
        m["pinv"] = C["pinv"][s]
        m["halom"] = C["halom"][s]
        m["relT"] = C["relT"][s]
        maps.append(m)
    return maps


_NC_CACHE = {}


def kernel(**inputs):
    maps = make_in_maps(inputs)
    if "nc" not in _NC_CACHE:
        _NC_CACHE["nc"] = build_program()
    res = run_bass_kernel_spmd(_NC_CACHE["nc"], maps, core_ids=list(range(8)))
    out = np.empty((4, L, D), np.float32)
    for core in range(8):
        b, s = core // 2, core % 2
        out[b, s * HALF:(s + 1) * HALF, :] = res.results[core]["yout"].T
    return out
```
